# Optimizing a Trainium2 kernel written in Bass

```python
import math
import jax, jax.numpy as jnp
from jax import lax
import numpy as np

D_MODEL = 2048
BATCH = 32
SEQ = 256
DEPTH = 2
DEC_BATCH = 2
DEC_SEQ = 1024
PAST_LEN = 512

GRID_W = 64
N_EVEN = (DEPTH + 1) // 2
N_ODD = DEPTH // 2
N_MOD = 9
NORM_EPS = 1e-6
D_FF = 5632
HEAD_DIM = 128
N_Q_HEADS = 8
N_KV_HEADS = 2
Q_PER_KV = N_Q_HEADS // N_KV_HEADS
ATTN_W = N_Q_HEADS * HEAD_DIM
KV_W = N_KV_HEADS * HEAD_DIM
Q_BLOCK = 128
ROPE_THETA = 10000.0
ROPE_AXIS_DIM = HEAD_DIM // 2
LRU_W = 1024
LRU_HEADS = 8
LRU_BLK = LRU_W // LRU_HEADS
LRU_CONV = 4
LRU_CONV_LEFT = 2
LRU_C = 8.0
MIX_W_AB = ATTN_W + LRU_W
IN_W_AB = ATTN_W + 2 * KV_W + 2 * LRU_W
HY_W = 1024
HY_ORDER = 2
HY_CONV = 3
HY_CONV_LEFT = 1
HY_EMB = 33
HY_BANDS = (HY_EMB - 1) // 2
HY_FH = 64
HY_DECAY_MIN = math.log(100.0) / 1.5
HY_DECAY_MAX = math.log(100.0) / 0.3
POOL_W = 1024
POOL_WINDOWS = (2, 4, 8, 16)
POOL_GROUPS = len(POOL_WINDOWS)
POOL_GW = POOL_W // POOL_GROUPS
MIX_W_CD = HY_W + POOL_W
IN_W_CD = (HY_ORDER + 1) * HY_W + POOL_W

kernel_name = 'hybrid_diffusion_prefix_trunk_step'

F32 = jnp.float32


def rmsnorm(x, g):
    xf = x.astype(F32)
    y = xf * lax.rsqrt(jnp.mean(xf * xf, axis=-1, keepdims=True) + NORM_EPS)
    return (y * g.astype(F32)).astype(x.dtype)


def modulation(cvec, w_mod, b_mod):
    m = jax.nn.silu(cvec) @ w_mod + b_mod
    return m.reshape(*cvec.shape[:-1], N_MOD, D_MODEL)


def modulate(x, g, m, j):
    return rmsnorm(x, g) * (1 + m[..., 3 * j + 1, :]) + m[..., 3 * j, :]


def swiglu(h, w13, w2):
    gate, up = jnp.split(h @ w13, 2, axis=-1)
    return (jax.nn.silu(gate) * up) @ w2


def ffn_sublayer(x, m, j, g, w13, w2):
    return x + 0.5 * m[..., 3 * j + 2, :] * swiglu(modulate(x, g, m, j), w13, w2)


def dwconv(x, w, b, left):
    k, L = w.shape[0], x.shape[1]
    xp = jnp.pad(x, ((0, 0), (left, k - 1 - left), (0, 0)))
    return sum(xp[:, j:j + L] * w[j] for j in range(k)) + b


def grid_rope(L):
    rows = L // GRID_W
    r = jnp.repeat(jnp.arange(rows), GRID_W).astype(F32)
    col = jnp.tile(jnp.arange(GRID_W), rows).astype(F32)
    inv = ROPE_THETA ** (-jnp.arange(0, ROPE_AXIS_DIM, 2, dtype=F32) / ROPE_AXIS_DIM)
    ang = jnp.concatenate([r[:, None] * inv, col[:, None] * inv], axis=-1)
    return jnp.cos(ang), jnp.sin(ang)


def apply_rope(x, cos, sin):
    xf = x.astype(F32).reshape(*x.shape[:-1], HEAD_DIM // 2, 2)
    x0, x1 = xf[..., 0], xf[..., 1]
    cs, sn = cos[None, :, None], sin[None, :, None]
    out = jnp.stack([x0 * cs - x1 * sn, x0 * sn + x1 * cs], axis=-1)
    return out.reshape(x.shape).astype(x.dtype)


def block_attention(q, k, v):
    B, T = q.shape[:2]
    nb = T // Q_BLOCK
    qb = jnp.moveaxis(q.reshape(B, nb, Q_BLOCK, N_KV_HEADS, Q_PER_KV, HEAD_DIM), 1, 0)
    scale = HEAD_DIM ** -0.5

    def one_block(qi):
        s = jnp.einsum('bqkgd,bskd->bkgqs', qi, k).astype(F32) * scale
        p = jax.nn.softmax(s, axis=-1).astype(v.dtype)
        return jnp.einsum('bkgqs,bskd->bqkgd', p, v)

    o = lax.map(one_block, qb)
    return jnp.moveaxis(o, 0, 1).reshape(B, T, ATTN_W)


def lru_combine(e1, e2):
    a1, b1 = e1
    a2, b2 = e2
    return a1 * a2, a2 * b1 + b2


def lru_scan(a, b, h0, reverse):
    edge = -1 if reverse else 0
    b = b.at[:, edge].add(a[:, edge] * h0)
    _, h = lax.associative_scan(lru_combine, (a, b), reverse=reverse, axis=1)
    return h


def rg_lru(xc, gate_w, gate_b, lam, h0_f, h0_b):
    B, L, _ = xc.shape
    xh = xc.reshape(B, L, LRU_HEADS, LRU_BLK)
    g = jnp.einsum('blhi,dghij->dgblhj', xh, gate_w).reshape(2, 2, B, L, LRU_W)
    g = jax.nn.sigmoid((g + gate_b[:, :, None, None]).astype(F32))
    r, i = g[:, 0], g[:, 1]
    log_a = -LRU_C * r * jax.nn.softplus(-lam.astype(F32))[:, None, None]
    a = jnp.exp(log_a)
    b = jnp.sqrt(-jnp.expm1(2.0 * log_a)) * (i * xc.astype(F32))
    hf = lru_scan(a[0], b[0], h0_f.astype(F32), False)
    hb = lru_scan(a[1], b[1], h0_b.astype(F32), True)
    return (hf + hb).astype(xc.dtype), hf[:, -1], hb[:, 0]


def mixer_ab(h, w_in, q_norm, k_norm, conv_w, conv_b, gate_w, gate_b, lam, w_out, ctx):
    B, L, _ = h.shape
    q, k, v, lx, lg = jnp.split(
        h @ w_in, [ATTN_W, ATTN_W + KV_W, ATTN_W + 2 * KV_W, ATTN_W + 2 * KV_W + LRU_W], axis=-1)
    q = rmsnorm(q.reshape(B, L, N_Q_HEADS, HEAD_DIM), q_norm)
    k = rmsnorm(k.reshape(B, L, N_KV_HEADS, HEAD_DIM), k_norm)
    v = v.reshape(B, L, N_KV_HEADS, HEAD_DIM)
    if ctx is None:
        k_all, v_all = k, v
        h0_f = h0_b = jnp.zeros((B, LRU_W), h.dtype)
    else:
        k_ctx, v_ctx, h0_f, h0_b = ctx
        cos, sin = grid_rope(L)
        q = apply_rope(q, cos, sin)
        k = apply_rope(k, cos, sin)
        k_all = jnp.concatenate([k_ctx, k], axis=1)
        v_all = jnp.concatenate([v_ctx, v], axis=1)
    attn = block_attention(q.reshape(B, L, N_KV_HEADS, Q_PER_KV, HEAD_DIM), k_all, v_all)
    xc = dwconv(lx, conv_w, conv_b, LRU_CONV_LEFT)
    rec, hf_last, hb_first = rg_lru(xc, gate_w, gate_b, lam, h0_f, h0_b)
    rec = jax.nn.gelu(lg) * rec
    out = jnp.concatenate([attn, rec], axis=-1) @ w_out
    if ctx is None:
        return out, (k, v, hf_last.astype(h.dtype), hb_first.astype(h.dtype))
    return out, None


def hyena_filters(L, w1, b1, w2, b2, w3, freq, decay):
    n = jnp.arange(L, dtype=F32)
    t = n / max(L - 1, 1)
    bands = jnp.linspace(1e-4, HY_BANDS - 1, HY_BANDS, dtype=F32)
    f = 2.0 * math.pi * n[:, None] * bands[None, :] / L
    z = jnp.concatenate([t[:, None], jnp.cos(f), -jnp.sin(f)], axis=-1).astype(w1.dtype)
    z = jnp.sin(freq[0] * (z @ w1 + b1))
    z = jnp.sin(freq[1] * (z @ w2 + b2))
    filt = (z @ w3).astype(F32) * jnp.exp(-t[:, None] * jnp.abs(decay.astype(F32)))
    filt = filt / jnp.sum(jnp.abs(filt), axis=0, keepdims=True)
    return filt.reshape(L, HY_ORDER, HY_W)


def sym_longconv(u, hk, d):
    L = u.shape[1]
    kern = jnp.concatenate([hk, jnp.zeros((1, hk.shape[1]), F32), hk[:0:-1]], axis=0)
    uf = jnp.fft.rfft(u.astype(F32), n=2 * L, axis=1)
    kf = jnp.fft.rfft(kern, n=2 * L, axis=0)
    y = jnp.fft.irfft(uf * kf[None], n=2 * L, axis=1)[:, :L]
    return (y + u.astype(F32) * d.astype(F32)).astype(u.dtype)


def multi_pool(x, pool_w, pool_scale):
    B, L, _ = x.shape
    xf = x.astype(F32)
    cs = jnp.concatenate([jnp.zeros((B, 1, POOL_W), F32), jnp.cumsum(xf, axis=1)], axis=1)
    t = jnp.arange(L)
    outs = []
    for gi, w in enumerate(POOL_WINDOWS):
        lo = jnp.clip(t - w // 2, 0, L)
        hi = jnp.clip(t + w // 2, 0, L)
        sl = slice(gi * POOL_GW, (gi + 1) * POOL_GW)
        csg = cs[:, :, sl]
        mean = (csg[:, hi] - csg[:, lo]) / (hi - lo).astype(F32)[None, :, None]
        outs.append((mean - xf[:, :, sl]).astype(x.dtype) @ pool_w[gi])
    return jnp.concatenate(outs, axis=-1) * pool_scale


def mixer_cd(h, w_in, conv_w, conv_b, fw1, fb1, fw2, fb2, fw3, ffreq, fdecay, fskip,
             pool_w, pool_scale, w_out):
    L = h.shape[1]
    proj = h @ w_in
    hy, pl = proj[..., :(HY_ORDER + 1) * HY_W], proj[..., (HY_ORDER + 1) * HY_W:]
    hy = dwconv(hy, conv_w, conv_b, HY_CONV_LEFT)
    x1, x2, v = jnp.split(hy, HY_ORDER + 1, axis=-1)
    filt = hyena_filters(L, fw1, fb1, fw2, fb2, fw3, ffreq, fdecay)
    z = x1 * sym_longconv(v, filt[:, 0], fskip[0])
    z = x2 * sym_longconv(z, filt[:, 1], fskip[1])
    out = jnp.concatenate([z, multi_pool(pl, pool_w, pool_scale)], axis=-1) @ w_out
    return out


def setup_inputs(seed: int = 0) -> dict:
    key = jax.random.key(seed)
    ks = iter(jax.random.split(key, 40))

    def nrm(shape, scale):
        return jax.random.normal(next(ks), shape, F32) * scale

    def gain(shape):
        return 1.0 + nrm(shape, 0.05)

    u = jax.random.uniform(next(ks), (N_EVEN, 2, LRU_W), F32, 0.9, 0.999)
    s = u ** (1.0 / LRU_C)
    lru_lambda = jnp.log(s) - jnp.log1p(-s)
    hy_decay = jax.random.uniform(next(ks), (N_ODD, HY_ORDER * HY_W), F32, HY_DECAY_MIN, HY_DECAY_MAX)
    return {
        'x_prompt': nrm((BATCH, SEQ, D_MODEL), 1.0),
        'x_sample': nrm((DEC_BATCH, DEC_SEQ, D_MODEL), 1.0),
        'cache_k': nrm((DEC_BATCH, N_EVEN, PAST_LEN, N_KV_HEADS, HEAD_DIM), 1.0),
        'cache_v': nrm((DEC_BATCH, N_EVEN, PAST_LEN, N_KV_HEADS, HEAD_DIM), 1.0),
        'state_lru_fwd': nrm((DEC_BATCH, N_EVEN, LRU_W), 0.5),
        'state_lru_bwd': nrm((DEC_BATCH, N_EVEN, LRU_W), 0.5),
        'c': nrm((DEC_BATCH, D_MODEL), 1.0),
        'c_ctx': nrm((D_MODEL,), 1.0),
        'norm_g': gain((DEPTH, 3, D_MODEL)),
        'w_mod': nrm((DEPTH, D_MODEL, N_MOD * D_MODEL), 0.5 * D_MODEL ** -0.5),
        'b_mod': nrm((DEPTH, N_MOD * D_MODEL), 0.01),
        'ffn_w13': nrm((DEPTH, 2, D_MODEL, 2 * D_FF), D_MODEL ** -0.5),
        'ffn_w2': nrm((DEPTH, 2, D_FF, D_MODEL), D_FF ** -0.5),
        'ab_w_in': nrm((N_EVEN, D_MODEL, IN_W_AB), D_MODEL ** -0.5),
        'ab_q_norm': gain((N_EVEN, HEAD_DIM)),
        'ab_k_norm': gain((N_EVEN, HEAD_DIM)),
        'lru_conv_w': nrm((N_EVEN, LRU_CONV, LRU_W), LRU_CONV ** -0.5),
        'lru_conv_b': nrm((N_EVEN, LRU_W), 0.01),
        'lru_gate_w': nrm((N_EVEN, 2, 2, LRU_HEADS, LRU_BLK, LRU_BLK), LRU_BLK ** -0.5),
        'lru_gate_b': nrm((N_EVEN, 2, 2, LRU_W), 0.01),
        'lru_lambda': lru_lambda,
        'ab_w_out': nrm((N_EVEN, MIX_W_AB, D_MODEL), MIX_W_AB ** -0.5),
        'cd_w_in': nrm((N_ODD, D_MODEL, IN_W_CD), D_MODEL ** -0.5),
        'hy_conv_w': nrm((N_ODD, HY_CONV, (HY_ORDER + 1) * HY_W), HY_CONV ** -0.5),
        'hy_conv_b': nrm((N_ODD, (HY_ORDER + 1) * HY_W), 0.01),
        'hy_w1': nrm((N_ODD, HY_EMB, HY_FH), HY_EMB ** -0.5),
        'hy_b1': nrm((N_ODD, HY_FH), 0.02),
        'hy_w2': nrm((N_ODD, HY_FH, HY_FH), HY_FH ** -0.5),
        'hy_b2': nrm((N_ODD, HY_FH), 0.02),
        'hy_w3': nrm((N_ODD, HY_FH, HY_ORDER * HY_W), HY_FH ** -0.5),
        'hy_freq': gain((N_ODD, 2, HY_FH)),
        'hy_decay': hy_decay,
        'hy_skip': nrm((N_ODD, HY_ORDER, HY_W), 0.5),
        'pool_w': nrm((N_ODD, POOL_GROUPS, POOL_GW, POOL_GW), POOL_GW ** -0.5),
        'pool_scale': gain((N_ODD, POOL_W)),
        'cd_w_out': nrm((N_ODD, MIX_W_CD, D_MODEL), MIX_W_CD ** -0.5),
    }


def reference(x_prompt, x_sample, cache_k, cache_v, state_lru_fwd, state_lru_bwd, c, c_ctx,
              norm_g, w_mod, b_mod, ffn_w13, ffn_w2,
              ab_w_in, ab_q_norm, ab_k_norm, lru_conv_w, lru_conv_b, lru_gate_w, lru_gate_b,
              lru_lambda, ab_w_out,
              cd_w_in, hy_conv_w, hy_conv_b, hy_w1, hy_b1, hy_w2, hy_b2, hy_w3, hy_freq,
              hy_decay, hy_skip, pool_w, pool_scale, cd_w_out):
    xp, xs = x_prompt, x_sample
    k_list, v_list, hf_list, hb_list = [], [], [], []
    for l in range(DEPTH):
        mp = modulation(c_ctx, w_mod[l], b_mod[l])[None, None]
        ms = modulation(c, w_mod[l], b_mod[l])[:, None]
        xp = ffn_sublayer(xp, mp, 0, norm_g[l, 0], ffn_w13[l, 0], ffn_w2[l, 0])
        xs = ffn_sublayer(xs, ms, 0, norm_g[l, 0], ffn_w13[l, 0], ffn_w2[l, 0])
        hp = modulate(xp, norm_g[l, 1], mp, 1)
        hs = modulate(xs, norm_g[l, 1], ms, 1)
        if l % 2 == 0:
            e = l // 2
            ab = (ab_w_in[e], ab_q_norm[e], ab_k_norm[e], lru_conv_w[e], lru_conv_b[e],
                  lru_gate_w[e], lru_gate_b[e], lru_lambda[e], ab_w_out[e])
            op, (kc, vc, hf, hb) = mixer_ab(hp, *ab, ctx=None)
            os_, _ = mixer_ab(hs, *ab, ctx=(cache_k[:, e], cache_v[:, e],
                                             state_lru_fwd[:, e], state_lru_bwd[:, e]))
            k_list.append(kc)
            v_list.append(vc)
            hf_list.append(hf)
            hb_list.append(hb)
        else:
            o = l // 2
            cd = (cd_w_in[o], hy_conv_w[o], hy_conv_b[o], hy_w1[o], hy_b1[o], hy_w2[o], hy_b2[o],
                  hy_w3[o], hy_freq[o], hy_decay[o], hy_skip[o], pool_w[o], pool_scale[o], cd_w_out[o])
            op = mixer_cd(hp, *cd)
            os_ = mixer_cd(hs, *cd)
        xp = xp + mp[..., 5, :] * op
        xs = xs + ms[..., 5, :] * os_
        xp = ffn_sublayer(xp, mp, 2, norm_g[l, 2], ffn_w13[l, 1], ffn_w2[l, 1])
        xs = ffn_sublayer(xs, ms, 2, norm_g[l, 2], ffn_w13[l, 1], ffn_w2[l, 1])
    new_cache_k = jnp.stack(k_list, axis=1)
    new_cache_v = jnp.stack(v_list, axis=1)
    new_state_lru_fwd = jnp.stack(hf_list, axis=1)
    new_state_lru_bwd = jnp.stack(hb_list, axis=1)
    return (xp, xs, new_cache_k, new_cache_v, new_state_lru_fwd, new_state_lru_bwd)
```

```python
import contextlib
import numpy as np
import ml_dtypes
import concourse.bass as bass
import concourse.mybir as mybir
from concourse.bass_utils import run_bass_kernel_spmd

F32 = mybir.dt.float32
BF16 = mybir.dt.bfloat16
AF = mybir.ActivationFunctionType
ALU = mybir.AluOpType

NCORES = 8
D = 2048
NCH = 16
T = 1280
TA = 1024
TB = 256
DFF = 5632
NHC = 44
GRP = 8
NST = 4
HOLD = 3
STAGE = 4096
ENGS = ['pe', 'act', 'dve', 'pool', 'sp']


class Prog:
    def __init__(self, nc, es, dry=False):
        self.nc, self.es, self.dry = nc, es, dry
        self.streams = {e: [] for e in ENGS}
        self.cnt = {e: 0 for e in ENGS}
        self.sem = {}
        self.dcnt = {}
        self.lastw = {}
        self.rd = {}
        self.waited = {e: {} for e in ENGS}
        self.out_tokens = []
        B = lambda *b: tuple(('bank', i) for i in b)
        self.alias = {('ps', 0): B(0, 1, 2), ('ps', 1): B(3, 4, 5), 'psx': B(6), 'psy': B(7),
                      'pS': B(0, 1), 'pO': B(2, 3), 'pD': B(4, 5),
                      'XC': (('ACTB', 0), ('ACTB', 1)), 'RB': (('ACTB', 2), ('ACTB', 3)),
                      'IB': (('ACTB', 4), ('ACTB', 5)), 'XCB': (('ACTB', 6),),
                      ('KVO', 0, 'k'): (('ACTB', 7), 'kvok'), ('KVO', 0, 'v'): (('ACTB', 7), 'kvov'),
                      'AB': ('RSTD',), 'HF': (('TMP', 0),), 'HB': (('TMP', 1),), 'REC': (('ACTB', 7),),
                      'XP': (('XPk', 0), ('XPk', 1)), 'PT0': (('XPk', 0),), 'PT1': (('XPk', 1),), 'ATT': (('TMP', 0),), 'RDEN': (('TMP', 1),),
                      'QN': (('TMP', 0),), 'T1': (('TMP', 1),), 'QTH': (('SQ', 0),),
                      'KT': (('ACTB', 0), ('ACTB', 1)), 'KTC': (('ACTB', 2),), 'VT': (('ACTB', 3), ('ACTB', 4), ('ACTB', 5)),
                      'CS': (('ACTB', 6), ('ACTB', 7)),
                      'pS0': B(0, 1), 'pS1': B(6, 7), ('MROW', 0): ('Z2T',), 'JUNK': (('TMP', 1),),
                      'RSTD': tuple(('RS', i) for i in range(5))}
        if not dry:
            for e in ['pe', 'act', 'dve', 'pool']:
                self.sem[e] = es.enter_context(nc.semaphore('s_' + e))

    def _exp(self, keys):
        out = []
        for k in keys:
            if k in self.alias:
                out.extend(self.alias[k])
            else:
                out.append(k)
        return tuple(out)

    serialize = False

    def _collect(self, eng, reads, writes):
        need = {}
        if self.serialize:
            for e2 in ('pe', 'act', 'dve', 'pool'):
                if self.cnt[e2] and not (e2 == 'pe' and eng == 'pe'):
                    need[e2] = self.cnt[e2]
            for k2, v2 in self.dcnt.items():
                need[k2] = v2
        def add(tok):
            if tok is None:
                return
            s, v = tok
            if s == 'pe' and eng == 'pe':
                return
            if need.get(s, 0) < v:
                need[s] = v
        for k in reads:
            add(self.lastw.get(k))
        for k in writes:
            add(self.lastw.get(k))
            for s, v in self.rd.get(k, {}).items():
                add((s, v))
        waits = []
        for s, v in need.items():
            if self.waited[eng].get(s, 0) >= v:
                continue
            self.waited[eng][s] = v
            waits.append((s, v))
        return waits

    def _commit(self, tok, reads, writes):
        s, v = tok
        for k in reads:
            d = self.rd.setdefault(k, {})
            if d.get(s, 0) < v:
                d[s] = v
        for k in writes:
            self.lastw[k] = tok
            self.rd[k] = {}

    sertags = ()

    def op(self, eng, fn, reads=(), writes=(), tag=None):
        if self.dry:
            return None
        self.serialize = tag is not None and tag in self.sertags
        reads, writes = self._exp(reads), self._exp(writes)
        waits = self._collect(eng, reads, writes)
        self.cnt[eng] += 1
        tok = (eng, self.cnt[eng])
        self.streams[eng].append((waits, fn, eng))
        self._commit(tok, reads, writes)
        return tok

    def dma(self, queue, fn, ndma, semname, reads=(), writes=(), is_output=False, tag=None):
        if self.dry:
            return None
        self.serialize = tag is not None and tag in self.sertags
        reads, writes = self._exp(reads), self._exp(writes)
        key = 'd:' + semname
        if key not in self.sem:
            self.sem[key] = self.es.enter_context(self.nc.semaphore('d_' + semname))
            self.dcnt[key] = 0
        waits = self._collect(queue, reads, writes)
        self.dcnt[key] += 16 * ndma
        tok = (key, self.dcnt[key])
        self.streams[queue].append((waits, fn, key))
        self._commit(tok, reads, writes)
        if is_output:
            self.out_tokens.append(tok)
        return tok

    def emit(self):
        nc = self.nc
        fin = {}
        for s, v in self.out_tokens:
            fin[s] = max(fin.get(s, 0), v)
        streams, sem = self.streams, self.sem

        def run(engname):
            def body(eng):
                for waits, fn, kind in streams[engname]:
                    for s, v in waits:
                        eng.wait_ge(sem[s], v)
                    if kind in ('pe', 'act', 'dve', 'pool'):
                        ins = fn(eng)
                        ins.then_inc(sem[kind], 1)
                    else:
                        fn(eng, sem[kind])
                if engname == 'sp':
                    for s, v in fin.items():
                        eng.wait_ge(sem[s], v)
            return body

        with nc.Block() as block:
            block.sync(run('sp'))
            block.tensor(run('pe'))
            block.scalar(run('act'))
            block.vector(run('dve'))
            block.gpsimd(run('pool'))


class WStream:
    def __init__(self, prog, wbuf, plan=None):
        self.p, self.wbuf = prog, wbuf
        self.plan = plan
        self.rec = []
        self.n = 0
        self.emitted = 0

    def _emit_dma(self, i):
        src, a, b = self.plan[i]
        slot = i % NST
        dst = self._view(slot, a[0] * a[1] if isinstance(a, tuple) else a, b)
        def fn(eng, sem, dst=dst, src=src):
            eng.dma_start(out=dst, in_=src).then_inc(sem, 16)
        self.p.dma('pool', fn, 1, 'w%d' % slot, reads=(), writes=(('w', slot),))

    def _view(self, slot, a, b):
        if isinstance(a, tuple):
            a0, a1 = a
            return self.wbuf[:, slot, 0:a0 * a1 * b].rearrange("p (a c b) -> p a c b", a=a0, c=a1)
        return self.wbuf[:, slot, 0:a * b].rearrange("p (a b) -> p a b", a=a)

    def next(self, src, a, b):
        i = self.n
        self.n += 1
        if self.p.dry:
            self.rec.append((src, a, b))
            return None, None
        while self.emitted < min(len(self.plan), i + NST - HOLD + 1):
            self._emit_dma(self.emitted)
            self.emitted += 1
        slot = i % NST
        return self._view(slot, a, b), ('w', slot)


class K:
    pass


def tok_tiles():
    return [(0, 512), (512, 512), (1024, 256)]


def build(stop=6):
    nc = bass.Bass("TRN2", target_bir_lowering=False)
    dr = {}

    def din(name, shape, dt=F32):
        dr[name] = nc.dram_tensor(name, list(shape), dt, kind="ExternalInput").ap()
        return dr[name]

    def dout(name, shape, dt=F32):
        dr[name] = nc.dram_tensor(name, list(shape), dt, kind="ExternalOutput").ap()
        return dr[name]

    din("xT", [D, T])
    din("cond", [128, 2, NCH])
    din("normg", [128, 6, NCH])
    din("bmod", [128, 2, 9 * NCH])
    din("w_mod", [2, D, 9 * D])
    din("ffn_w13", [2, 2, D, 2 * DFF])
    din("ffn_w2", [2, 2, DFF, D])
    din("ab_w_in", [1, D, 3584])
    din("knb", [128, 128])
    din("lru_gate_w", [8, 128, 4, 128])
    din("lruc", [128, 8, 10])
    din("lrul", [128, 2, 8])
    din("lruh", [128, 2, 8])
    din("cont", [128, 1])
    dout("sto", [128, 8 * 5 * 2])
    din("ab_w_out", [1, D, D])
    din("qkn", [128, 2])
    din("rotm", [128, 128])
    din("cossin", [128, 2, TA])
    din("ckT", [128, 2, 512])
    din("cv", [128, 4, 256])
    din("maskb", [128, 48])
    din("poolD", [4, 128, 30, 128], BF16)
    din("pool_w", [1, 4, 256, 256])
    din("pscl", [128, 8])
    din("cd_w_in", [1, D, 4096])
    din("cd_w_out", [1, D, D])
    din("hy_w1", [1, 33, 64])
    din("hy_w2", [1, 64, 64])
    din("hy_w3", [1, 64, 2048])
    din("hyfb", [64, 4])
    din("hyc", [128, 24, 4])
    din("hsk", [128, 2, 8])
    din("decb", [128, 2048])
    din("z0T", [33, T])
    din("negt", [128, 10])
    din("ffA", [TA, 2048], BF16)
    din("gtA", [TA, 2048], BF16)
    din("fiA", [2048, TA], BF16)
    din("ffB", [TB, 512], BF16)
    din("gtB", [TB, 512], BF16)
    din("fiB", [512, TB], BF16)
    dout("kvo", [T, 512])
    dout("yT", [D, T])
    dout("modout", [128, 2 * 9 * NCH * 2])

    with contextlib.ExitStack() as es:
        X = es.enter_context(nc.sbuf_tensor("X", [128, NCH, T], F32))
        H = es.enter_context(nc.sbuf_tensor("H", [128, NCH, T], BF16))
        ACTB = es.enter_context(nc.sbuf_tensor("ACTB", [128, GRP, T], BF16))
        WBUF = es.enter_context(nc.sbuf_tensor("WBUF", [128, NST, STAGE], BF16))
        SQ = es.enter_context(nc.sbuf_tensor("SQ", [128, 1, T], BF16))
        RSTD = es.enter_context(nc.sbuf_tensor("RSTD", [128, T], F32))
        TMP = es.enter_context(nc.sbuf_tensor("TMP", [128, 2, T], F32))
        MOD = es.enter_context(nc.sbuf_tensor("MOD", [128, 2, 9, NCH, 2], F32))
        MA = es.enter_context(nc.sbuf_tensor("MA", [128, 2, 3, NCH, 2], F32))
        MG = es.enter_context(nc.sbuf_tensor("MG", [128, 2, 3, NCH, 2], F32))
        NG = es.enter_context(nc.sbuf_tensor("NG", [128, 6, NCH], F32))
        BM = es.enter_context(nc.sbuf_tensor("BM", [128, 2, 9 * NCH], F32))
        COND = es.enter_context(nc.sbuf_tensor("COND", [128, 2, NCH], F32))
        ST = es.enter_context(nc.sbuf_tensor("ST", [128, NCH, 2], BF16))
        ONES = es.enter_context(nc.sbuf_tensor("ONES", [128, 128], BF16))
        IDF = es.enter_context(nc.sbuf_tensor("IDF", [128, 128], F32))
        PS = es.enter_context(nc.psum_tensor("PS", [128, 8 * 512], F32))
        KNB = es.enter_context(nc.sbuf_tensor("KNB", [128, 128], F32))
        Z2T = es.enter_context(nc.sbuf_tensor("Z2T", [64, T], BF16))
        MROW = Z2T[0:2, 0:1024].bitcast(F32).rearrange("p (o c) -> p o c", o=1)
        W3S = es.enter_context(nc.sbuf_tensor("W3S", [64, 256], BF16))
        HW1 = es.enter_context(nc.sbuf_tensor("HW1", [33, 64], F32))
        HW2 = es.enter_context(nc.sbuf_tensor("HW2", [64, 64], F32))
        HFB = es.enter_context(nc.sbuf_tensor("HFB", [64, 8], F32))
        NEGT = es.enter_context(nc.sbuf_tensor("NEGT", [128, 10], F32))
        HYC = es.enter_context(nc.sbuf_tensor("HYC", [128, 24, 4], F32))
        HSK = es.enter_context(nc.sbuf_tensor("HSK", [128, 2, 8], F32))
        IDB = es.enter_context(nc.sbuf_tensor("IDB", [128, 128], BF16))
        PSCL = es.enter_context(nc.sbuf_tensor("PSCL", [128, 8], F32))
        QKN = es.enter_context(nc.sbuf_tensor("QKN", [128, 2], F32))
        ROTM = es.enter_context(nc.sbuf_tensor("ROTM", [128, 128], F32))
        MASKB = es.enter_context(nc.sbuf_tensor("MASKB", [128, 48], F32))
        LRUC = es.enter_context(nc.sbuf_tensor("LRUC", [128, 8, 10], F32))
        LRUL = es.enter_context(nc.sbuf_tensor("LRUL", [128, 2, 8], F32))
        LRUL2 = es.enter_context(nc.sbuf_tensor("LRUL2", [128, 2, 8], F32))
        LRUH = es.enter_context(nc.sbuf_tensor("LRUH", [128, 2, 8], F32))
        CONT = es.enter_context(nc.sbuf_tensor("CONT", [128, 1], F32))
        INIT = es.enter_context(nc.sbuf_tensor("INIT", [128, 8], F32))
        STO = es.enter_context(nc.sbuf_tensor("STO", [128, 8, 5, 2], F32))
        XP = es.enter_context(nc.sbuf_tensor("XP", [128, 5, 259], F32))
        def actb_f32(i):
            return ACTB[:, 2 * i:2 * i + 2, :].rearrange("p a t -> p (a t)").bitcast(F32).rearrange("p (s t) -> p s t", s=5)
        XC, RB, IB = actb_f32(0), actb_f32(1), actb_f32(2)
        XCB = ACTB[:, 6, :]
        KVO = ACTB[:, 7, 0:1024].bitcast(F32).rearrange("p (o c) -> p o c", o=1)
        SS = es.enter_context(nc.sbuf_tensor("SS", [128, 4], F32))
        JUNK = TMP[:, 1, 0:128]

        def ps_tiles(slot):
            base = 3 * slot * 512
            return [PS[:, base:base + 512], PS[:, base + 512:base + 1024], PS[:, base + 1024:base + 1280]]

        def ps_AB(slot):
            base = 3 * slot * 512
            return PS[:, base:base + 1024], PS[:, base + 1024:base + 1280]

        PSX = PS[:, 6 * 512: 7 * 512]
        PSY = PS[:, 7 * 512: 8 * 512]

        plan = None
        for dry in (True, False):
            p = Prog(nc, es, dry=dry)
            ws = WStream(p, WBUF, plan)
            slotc = [0]

            def new_slot():
                s = slotc[0] % 2
                slotc[0] += 1
                return s

            def ld(dst, src, key, sem):
                def fn(eng, sm, dst=dst, src=src):
                    eng.dma_start(out=dst, in_=src).then_inc(sm, 16)
                p.dma('sp', fn, 1, sem, writes=(key,))

            xv = dr["xT"].rearrange("(c p) t -> p c t", p=128)
            for q in range(4):
                ld(X[:, 4 * q:4 * q + 4, :], xv[:, 4 * q:4 * q + 4, :], ('X', q), 'ldx%d' % q)
            for q in range(4):
                for c in range(4 * q, 4 * q + 4):
                    if not dry:
                        p.lastw[('Xc', c)] = p.lastw[('X', q)]
            cl_ = [(COND[:], dr["cond"], 'COND'), (NG[:], dr["normg"], 'NG'), (BM[:], dr["bmod"], 'BM'),
                   (KNB[:], dr["knb"], 'KNB'), (HSK[:], dr["hsk"], 'HSK'), (PSCL[:], dr["pscl"], 'PSCL'),
                   (QKN[:], dr["qkn"], 'QKN'), (ROTM[:], dr["rotm"], 'ROTM'), (MASKB[:], dr["maskb"], 'MASKB'),
                   (LRUC[:], dr["lruc"], 'LRUC'), (LRUL[:], dr["lrul"], 'LRUL'), (LRUH[:], dr["lruh"], 'LRUH'),
                   (CONT[:], dr["cont"], 'CONT')]
            def ldall(eng, sm):
                for dst_, src_, _ in cl_:
                    eng.dma_start(out=dst_, in_=src_).then_inc(sm, 16)
            p.dma('sp', ldall, len(cl_), 'ldc', writes=tuple(k_ for _, _, k_ in cl_))
            p.op('dve', lambda e: e.memset(XP[:], 0.0), writes=('XP',))
            p.op('dve', lambda e: e.memset(ONES[:], 1.0), writes=('ONES',))
            p.op('dve', lambda e: e.memset(IDF[:], 0.0), writes=('IDF',))
            def mk_ident(e):
                return e.affine_select(out=IDF[:], in_=IDF[:], pattern=[[-1, 128]], compare_op=ALU.not_equal,
                                       fill=1.0, base=0, channel_multiplier=1)
            p.op('pool', mk_ident, reads=('IDF',), writes=('IDF',))
            p.op('act', lambda e: e.copy(out=IDB[:], in_=IDF[:]), reads=('IDF',), writes=('IDB',))

            p.op('act', lambda e: e.activation(out=ST[:].rearrange("p c k -> p k c"), in_=COND[:], func=AF.Silu),
                 reads=('COND',), writes=('ST',))

            def modulation(l, qlo, qhi):
                for cb in range(qlo * 4, qhi * 4):
                    mod_block(l, cb)

            def mod_tasks(l, qlo, qhi):
                return [(lambda l=l, cb=cb: mod_block(l, cb)) for cb in range(qlo * 4, qhi * 4)]

            def mod_block(l, cb):
                for _ in range(1):
                    stg = []
                    for half in range(2):
                        src = dr["w_mod"][l, half * 1024:(half + 1) * 1024, cb * 512:(cb + 1) * 512] \
                            .rearrange("(k p) c -> p k c", p=128)
                        stg.append(ws.next(src, 8, 512))
                    if dry:
                        continue
                    pt = PSY
                    def mm(e, stg=stg, pt=pt):
                        ins = None
                        for kc in range(16):
                            v, _ = stg[kc // 8]
                            ins = e.matmul(pt[0:2, :], ST[:, kc, :], v[:, kc % 8, :], start=(kc == 0), stop=(kc == 15))
                        return ins
                    p.op('pe', mm, reads=('ST', stg[0][1], stg[1][1]), writes=('psy',))
                    par = 0
                    p.op('act', lambda e, pt=pt, par=par: e.copy(out=MROW[:, par, :], in_=pt[0:2, :]),
                         reads=('psy',), writes=(('MROW', par),))
                    def tr(e, par=par):
                        ins = None
                        for i in range(4):
                            ins = e.transpose(PSX[:, 2 * i:2 * i + 2], MROW[:, par, i * 128:(i + 1) * 128], IDF[0:2, 0:2])
                        return ins
                    p.op('pe', tr, reads=(('MROW', par), 'IDF'), writes=('psx',))
                    q, c0 = cb // 4, (cb % 4) * 4
                    def ev(e, q=q, c0=c0, l=l):
                        return e.tensor_tensor(
                            out=MOD[:, l, q, c0:c0 + 4, :],
                            in0=PSX[:, 0:8].rearrange("p (c k) -> p c k", k=2),
                            in1=BM[:, l, q * NCH + c0:q * NCH + c0 + 4].unsqueeze(2).to_broadcast([128, 4, 2]),
                            op=ALU.add)
                    p.op('dve', ev, reads=('psx', 'BM'), writes=(('MOD', l, q),))

            def mod_derive(l, j):
                def f1(e, l=l, j=j):
                    return e.scalar_tensor_tensor(
                        out=MA[:, l, j], in0=MOD[:, l, 3 * j + 1], scalar=1.0,
                        in1=NG[:, l * 3 + j, :].unsqueeze(2).to_broadcast([128, NCH, 2]),
                        op0=ALU.add, op1=ALU.mult)
                p.op('dve', f1, reads=(('MOD', l, 3 * j + 1), 'NG'), writes=(('MA', l, j),))
                sc = 1.0 if j == 1 else 0.5
                p.op('dve', lambda e, l=l, j=j, sc=sc: e.tensor_scalar(
                    out=MG[:, l, j], in0=MOD[:, l, 3 * j + 2], scalar1=sc, scalar2=None, op0=ALU.mult),
                    reads=(('MOD', l, 3 * j + 2),), writes=(('MG', l, j),))

            def norm_modulate(l, j):
                slot = new_slot()
                pts = ps_tiles(slot)
                for c in range(NCH):
                    par = 0
                    p.op('act', lambda e, c=c, par=par: e.activation(out=SQ[:, par, :], in_=X[:, c, :], func=AF.Square),
                         reads=(('Xc', c),), writes=(('SQ', par),))
                    def mm(e, c=c, par=par):
                        ins = None
                        for ti, (t0, tw) in enumerate(tok_tiles()):
                            ins = e.matmul(pts[ti], ONES[:], SQ[:, par, t0:t0 + tw], start=(c == 0), stop=(c == NCH - 1))
                        return ins
                    p.op('pe', mm, reads=(('SQ', par), 'ONES'), writes=(('ps', slot),))
                pa, pb = ps_AB(slot)
                def ln(e):
                    e.activation(out=RSTD[:, 0:TA], in_=pa, func=AF.Ln, scale=1.0 / D, bias=1e-6)
                    return e.activation(out=RSTD[:, TA:T], in_=pb, func=AF.Ln, scale=1.0 / D, bias=1e-6)
                p.op('act', ln, reads=(('ps', slot),), writes=('RSTD',))
                p.op('act', lambda e: e.activation(out=RSTD[:], in_=RSTD[:], func=AF.Exp, scale=-0.5),
                     reads=('RSTD',), writes=('RSTD',))
                for c in range(NCH):
                    par = c % 2
                    def f1(e, c=c, par=par):
                        e.scalar_tensor_tensor(out=TMP[:, par, 0:TA], in0=X[:, c, 0:TA], scalar=MA[:, l, j, c, 0:1],
                                               in1=RSTD[:, 0:TA], op0=ALU.mult, op1=ALU.mult)
                        return e.scalar_tensor_tensor(out=TMP[:, par, TA:T], in0=X[:, c, TA:T], scalar=MA[:, l, j, c, 1:2],
                                                      in1=RSTD[:, TA:T], op0=ALU.mult, op1=ALU.mult)
                    p.op('dve', f1, reads=(('Xc', c), 'RSTD', ('MA', l, j)), writes=(('TMP', par),))
                    def f2(e, c=c, par=par):
                        e.activation(out=H[:, c, 0:TA], in_=TMP[:, par, 0:TA], func=AF.Identity,
                                     bias=MOD[:, l, 3 * j, c, 0:1])
                        return e.activation(out=H[:, c, TA:T], in_=TMP[:, par, TA:T], func=AF.Identity,
                                            bias=MOD[:, l, 3 * j, c, 1:2])
                    p.op('act', f2, reads=(('TMP', par), ('MOD', l, 3 * j)), writes=(('H', c),))

            def ffn(l, j, fi, bg=None):
                bg = list(bg or [])
                norm_modulate(l, j)
                w13 = dr["ffn_w13"][l, fi]
                w2 = dr["ffn_w2"][l, fi]
                Hkeys = tuple(('H', c) for c in range(NCH))
                hc0 = 0
                while hc0 < NHC:
                    ng = min(GRP, NHC - hc0)
                    for pr in range(ng // 2):
                        i0 = hc0 + 2 * pr
                        gsrc = w13[:, i0 * 128:(i0 + 2) * 128].rearrange("(k p) c -> p k c", p=128)
                        usrc = w13[:, DFF + i0 * 128:DFF + (i0 + 2) * 128].rearrange("(k p) c -> p k c", p=128)
                        gv, gk = ws.next(gsrc, 16, 256)
                        uv, uk = ws.next(usrc, 16, 256)
                        if dry:
                            if bg:
                                bg.pop(0)()
                            continue
                        for sub in range(2):
                            li = 2 * pr + sub
                            sg, su = new_slot(), new_slot()
                            for (sl, wv, wk) in ((sg, gv, gk), (su, uv, uk)):
                                pts = ps_tiles(sl)
                                def mm(e, wv=wv, pts=pts, sub=sub):
                                    ins = None
                                    for kc in range(NCH):
                                        for ti, (t0, tw) in enumerate(tok_tiles()):
                                            ins = e.matmul(pts[ti], wv[:, kc, sub * 128:(sub + 1) * 128], H[:, kc, t0:t0 + tw],
                                                           start=(kc == 0), stop=(kc == NCH - 1))
                                    return ins
                                p.op('pe', mm, reads=Hkeys + (wk,), writes=(('ps', sl),))
                            ga, gb = ps_AB(sg)
                            ua, ub = ps_AB(su)
                            par = li % 2
                            def silu(e, ga=ga, gb=gb, par=par):
                                e.activation(out=TMP[:, par, 0:TA], in_=ga, func=AF.Silu)
                                return e.activation(out=TMP[:, par, TA:T], in_=gb, func=AF.Silu)
                            p.op('act', silu, reads=(('ps', sg),), writes=(('TMP', par),))
                            def mul(e, ua=ua, ub=ub, par=par, li=li):
                                e.tensor_tensor(out=ACTB[:, li, 0:TA], in0=TMP[:, par, 0:TA], in1=ua, op=ALU.mult)
                                return e.tensor_tensor(out=ACTB[:, li, TA:T], in0=TMP[:, par, TA:T], in1=ub, op=ALU.mult)
                            p.op('dve', mul, reads=(('TMP', par), ('ps', su)), writes=(('ACTB', li),))
                        if bg:
                            bg.pop(0)()
                    Akeys = tuple(('ACTB', i) for i in range(ng))
                    for ob in range(4):
                        src = w2[hc0 * 128:(hc0 + ng) * 128, ob * 512:(ob + 1) * 512].rearrange("(k p) c -> p k c", p=128)
                        wv, wk = ws.next(src, ng, 512)
                        if dry:
                            if bg:
                                bg.pop(0)()
                            continue
                        for oc in range(4):
                            c = ob * 4 + oc
                            sl = new_slot()
                            pts = ps_tiles(sl)
                            def mm(e, wv=wv, pts=pts, oc=oc, ng=ng):
                                ins = None
                                for kc in range(ng):
                                    for ti, (t0, tw) in enumerate(tok_tiles()):
                                        ins = e.matmul(pts[ti], wv[:, kc, oc * 128:(oc + 1) * 128], ACTB[:, kc, t0:t0 + tw],
                                                       start=(kc == 0), stop=(kc == ng - 1))
                                return ins
                            p.op('pe', mm, reads=Akeys + (wk,), writes=(('ps', sl),))
                            oa, ob_ = ps_AB(sl)
                            def acc(e, oa=oa, ob_=ob_, c=c):
                                e.scalar_tensor_tensor(out=X[:, c, 0:TA], in0=oa, scalar=MG[:, l, j, c, 0:1],
                                                       in1=X[:, c, 0:TA], op0=ALU.mult, op1=ALU.add)
                                return e.scalar_tensor_tensor(out=X[:, c, TA:T], in0=ob_, scalar=MG[:, l, j, c, 1:2],
                                                              in1=X[:, c, TA:T], op0=ALU.mult, op1=ALU.add)
                            p.op('dve', acc, reads=(('ps', sl), ('Xc', c), ('MG', l, j)), writes=(('Xc', c),))
                        if bg:
                            bg.pop(0)()
                    hc0 += ng
                while bg:
                    bg.pop(0)()

            def kv_cache_out(e_idx):
                win = dr["ab_w_in"][e_idx]
                stg = []
                for half in range(2):
                    src = win[half * 1024:(half + 1) * 1024, 1024:1536].rearrange("(k p) c -> p k c", p=128)
                    stg.append(ws.next(src, 8, 512))
                if dry:
                    return
                Hkeys = tuple(('H', c) for c in range(NCH))
                for tt in range(T // 128):
                    par = 0
                    bank = PSX if tt % 2 == 0 else PSY
                    bkey = 'psx' if tt % 2 == 0 else 'psy'
                    def mm(e, tt=tt, bank=bank):
                        ins = None
                        for kc in range(NCH):
                            v, _ = stg[kc // 8]
                            ins = e.matmul(bank, H[:, kc, tt * 128:(tt + 1) * 128], v[:, kc % 8, :],
                                           start=(kc == 0), stop=(kc == NCH - 1))
                        return ins
                    p.op('pe', mm, reads=Hkeys + (stg[0][1], stg[1][1]), writes=(bkey,))
                    p.op('act', lambda e, bank=bank, par=par: e.copy(out=KVO[:, par, 256:512], in_=bank[:, 256:512]),
                         reads=(bkey,), writes=(('KVO', par, 'v'),))
                    def sq(e, bank=bank, par=par):
                        e.activation(out=JUNK, in_=bank[:, 0:128], func=AF.Square, accum_out=SS[:, 2 * par:2 * par + 1])
                        return e.activation(out=JUNK, in_=bank[:, 128:256], func=AF.Square, accum_out=SS[:, 2 * par + 1:2 * par + 2])
                    p.op('act', sq, reads=(bkey,), writes=('JUNK', ('SS', par)))
                    p.op('act', lambda e, par=par: e.activation(out=SS[:, 2 * par:2 * par + 2], in_=SS[:, 2 * par:2 * par + 2],
                                                               func=AF.Ln, scale=1.0 / 128, bias=1e-6),
                         reads=(('SS', par),), writes=(('SS', par),))
                    p.op('act', lambda e, par=par: e.activation(out=SS[:, 2 * par:2 * par + 2], in_=SS[:, 2 * par:2 * par + 2],
                                                               func=AF.Exp, scale=-0.5),
                         reads=(('SS', par),), writes=(('SS', par),))
                    def kn(e, bank=bank, par=par):
                        e.scalar_tensor_tensor(out=KVO[:, par, 0:128], in0=bank[:, 0:128], scalar=SS[:, 2 * par:2 * par + 1],
                                               in1=KNB[:], op0=ALU.mult, op1=ALU.mult)
                        return e.scalar_tensor_tensor(out=KVO[:, par, 128:256], in0=bank[:, 128:256],
                                                      scalar=SS[:, 2 * par + 1:2 * par + 2], in1=KNB[:], op0=ALU.mult, op1=ALU.mult)
                    p.op('dve', kn, reads=(bkey, ('SS', par), 'KNB'), writes=(('KVO', par, 'k'),))
                    def st(eng, sm, tt=tt, par=par):
                        eng.dma_start(out=dr["kvo"][tt * 128:(tt + 1) * 128, :], in_=KVO[:, par, :]).then_inc(sm, 16)
                    p.dma('sp', st, 1, 'stkv%d' % par, reads=(('KVO', par, 'k'), ('KVO', par, 'v')), is_output=True)

            def wout_incr(rows_ap, act_ap, act_key, l):
                acts = act_ap if isinstance(act_ap, (list, tuple)) else [act_ap]
                keys = tuple(act_key) if isinstance(act_key, list) else (act_key,)
                nk = rows_ap.shape[0] // 128
                wv, wk = ws.next(rows_ap.rearrange("(k p) c -> p k c", p=128), nk, D)
                if dry:
                    return
                for oc in range(NCH):
                    sl = new_slot()
                    pts = ps_tiles(sl)
                    def mm(e, wv=wv, pts=pts, oc=oc):
                        ins = None
                        for k in range(nk):
                            for ti, (t0, tw) in enumerate(tok_tiles()):
                                ins = e.matmul(pts[ti], wv[:, k, oc * 128:(oc + 1) * 128], acts[k][:, t0:t0 + tw],
                                               start=(k == 0), stop=(k == nk - 1))
                        return ins
                    p.op('pe', mm, reads=keys + (wk,), writes=(('ps', sl),))
                    oa, ob_ = ps_AB(sl)
                    def acc(e, oa=oa, ob_=ob_, oc=oc):
                        e.scalar_tensor_tensor(out=X[:, oc, 0:TA], in0=oa, scalar=MG[:, l, 1, oc, 0:1],
                                               in1=X[:, oc, 0:TA], op0=ALU.mult, op1=ALU.add)
                        return e.scalar_tensor_tensor(out=X[:, oc, TA:T], in0=ob_, scalar=MG[:, l, 1, oc, 1:2],
                                                      in1=X[:, oc, TA:T], op0=ALU.mult, op1=ALU.add)
                    p.op('dve', acc, reads=(('ps', sl), ('Xc', oc), ('MG', l, 1)), writes=(('Xc', oc),))

            def lru(e_idx):
                win = dr["ab_w_in"][e_idx]
                Hkeys = tuple(('H', c) for c in range(NCH))
                p.op('act', lambda e: e.activation(out=LRUL[:], in_=LRUL[:], func=AF.Exp, scale=-1.0),
                     reads=('LRUL',), writes=('LRUL',))
                p.op('act', lambda e: e.activation(out=LRUL[:], in_=LRUL[:], func=AF.Ln, bias=1.0),
                     reads=('LRUL',), writes=('LRUL',))
                p.op('dve', lambda e: e.tensor_scalar(out=LRUL2[:], in0=LRUL[:], scalar1=-16.0, scalar2=None, op0=ALU.mult),
                     reads=('LRUL',), writes=('LRUL2',))
                p.op('dve', lambda e: e.tensor_scalar(out=LRUL[:], in0=LRUL[:], scalar1=-8.0, scalar2=None, op0=ALU.mult),
                     reads=('LRUL', 'LRUL2'), writes=('LRUL',))
                AB = RSTD[:].rearrange("p (s t) -> p s t", s=5)
                HF = TMP[:, 0, :].rearrange("p (s t) -> p s t", s=5)
                HB = TMP[:, 1, :].rearrange("p (s t) -> p s t", s=5)
                for c in range(8):
                    src = win[:, 1536 + c * 128:1536 + (c + 1) * 128].rearrange("(k p) c -> p k c", p=128)
                    wv, wk = ws.next(src, 16, 128)
                    gw, gwk = ws.next(dr["lru_gate_w"][c], 4, 128)
                    if dry:
                        ws.next(win[:, 2560 + c * 128:2560 + (c + 1) * 128].rearrange("(k p) c -> p k c", p=128), 16, 128)
                        wout_incr(dr["ab_w_out"][e_idx][1024 + c * 128:1024 + (c + 1) * 128, :], None, None, 0)
                        continue
                    for sub in range(1):
                        sl = new_slot()
                        pts = ps_tiles(sl)
                        def mm(e, wv=wv, pts=pts, sub=sub):
                            ins = None
                            for kc in range(NCH):
                                for ti, (t0, tw) in enumerate(tok_tiles()):
                                    ins = e.matmul(pts[ti], wv[:, kc, :], H[:, kc, t0:t0 + tw],
                                                   start=(kc == 0), stop=(kc == NCH - 1))
                            return ins
                        p.op('pe', mm, reads=Hkeys + (wk,), writes=(('ps', sl),))
                        pa, pb = ps_AB(sl)
                        def cp_in(e, pa=pa, pb=pb):
                            e.copy(out=XP[:, 0:4, 2:258], in_=pa.rearrange("p (s t) -> p s t", s=4))
                            return e.copy(out=XP[:, 4, 2:258], in_=pb)
                        p.op('act', cp_in, reads=(('ps', sl),), writes=('XP',))
                        def halo(e):
                            e.tensor_scalar(out=XP[:, 1:4, 0:2], in0=XP[:, 0:3, 256:258], scalar1=CONT[:, 0:1], scalar2=None, op0=ALU.mult)
                            return e.tensor_scalar(out=XP[:, 0:3, 258:259], in0=XP[:, 1:4, 2:3], scalar1=CONT[:, 0:1], scalar2=None, op0=ALU.mult)
                        p.op('dve', halo, reads=('XP', 'CONT'), writes=('XP',))
                        def conv(e, c=c):
                            e.tensor_scalar(out=XC, in0=XP[:, :, 0:256], scalar1=LRUC[:, c, 0:1], scalar2=LRUC[:, c, 4:5],
                                            op0=ALU.mult, op1=ALU.add)
                            ins = None
                            for jt in range(1, 4):
                                ins = e.scalar_tensor_tensor(out=XC, in0=XP[:, :, jt:jt + 256], scalar=LRUC[:, c, jt:jt + 1],
                                                             in1=XC, op0=ALU.mult, op1=ALU.add)
                            return ins
                        p.op('dve', conv, reads=('XP', 'LRUC'), writes=('XC',))
                        p.op('act', lambda e: e.copy(out=XCB, in_=XC.rearrange("p s t -> p (s t)")),
                             reads=('XC',), writes=('XCB',))
                        for d in range(2):
                            outs = (RB, IB)
                            for g in range(2):
                                sl = new_slot()
                                pts = ps_tiles(sl)
                                gi = d * 2 + g
                                def gm(e, pts=pts, gi=gi, gw=gw, sub=sub):
                                    ins = None
                                    for ti, (t0, tw) in enumerate(tok_tiles()):
                                        ins = e.matmul(pts[ti], gw[:, gi, :], XCB[:, t0:t0 + tw], start=True, stop=True)
                                    return ins
                                p.op('pe', gm, reads=('XCB', gwk), writes=(('ps', sl),))
                                pa, pb = ps_AB(sl)
                                dst = outs[g]
                                def sg(e, pa=pa, pb=pb, dst=dst, c=c, d=d, g=g):
                                    bias = LRUC[:, c, 5 + d * 2 + g:6 + d * 2 + g]
                                    e.activation(out=dst[:, 0:4, :], in_=pa.rearrange("p (s t) -> p s t", s=4), func=AF.Sigmoid, bias=bias)
                                    return e.activation(out=dst[:, 4, :], in_=pb, func=AF.Sigmoid, bias=bias)
                                p.op('act', sg, reads=(('ps', sl), 'LRUC'), writes=('RB' if g == 0 else 'IB',))
                            p.op('act', lambda e, c=c, d=d: e.activation(out=AB, in_=RB, func=AF.Exp, scale=LRUL[:, d, c:c + 1]),
                                 reads=('RB', 'LRUL'), writes=('AB',))
                            p.op('act', lambda e, c=c, d=d: e.activation(out=RB, in_=RB, func=AF.Exp, scale=LRUL2[:, d, c:c + 1]),
                                 reads=('RB', 'LRUL2'), writes=('RB',))
                            p.op('act', lambda e: e.activation(out=RB, in_=RB, func=AF.Sqrt, scale=-1.0, bias=1.0),
                                 reads=('RB',), writes=('RB',))
                            def bmul(e):
                                e.tensor_tensor(out=IB, in0=IB, in1=XC, op=ALU.mult)
                                return e.tensor_tensor(out=IB, in0=IB, in1=RB, op=ALU.mult)
                            p.op('dve', bmul, reads=('IB', 'XC', 'RB'), writes=('IB',))
                            HD = HF if d == 0 else HB
                            hk = 'HF' if d == 0 else 'HB'
                            order = range(5) if d == 0 else range(4, -1, -1)
                            for sgi in order:
                                init = None
                                if d == 0:
                                    if sgi == 0:
                                        init = LRUH[:, 0, c:c + 1]
                                    elif sgi == 4:
                                        init = 0.0
                                    else:
                                        src_col = HD[:, sgi - 1, 255:256]
                                else:
                                    if sgi == 4:
                                        init = 0.0
                                    elif sgi == 3:
                                        init = LRUH[:, 1, c:c + 1]
                                    else:
                                        src_col = HD[:, sgi + 1, 0:1]
                                if init is None:
                                    p.op('dve', lambda e, sgi=sgi, src_col=src_col: e.tensor_scalar(
                                        out=INIT[:, sgi:sgi + 1], in0=src_col, scalar1=CONT[:, 0:1], scalar2=None, op0=ALU.mult),
                                        reads=(hk, 'CONT'), writes=('INIT',))
                                    init = INIT[:, sgi:sgi + 1]
                                if d == 0:
                                    fn = lambda e, sgi=sgi, init=init, HD=HD: e.tensor_tensor_scan(
                                        out=HD[:, sgi, :], data0=AB[:, sgi, :], data1=IB[:, sgi, :], initial=init, op0=ALU.mult, op1=ALU.add)
                                else:
                                    fn = lambda e, sgi=sgi, init=init, HD=HD: e.tensor_tensor_scan(
                                        out=HD[:, sgi, ::-1], data0=AB[:, sgi, ::-1], data1=IB[:, sgi, ::-1], initial=init,
                                        op0=ALU.mult, op1=ALU.add)
                                p.op('dve', fn, reads=('AB', 'IB', 'LRUH', 'INIT'), writes=(hk,))
                            col = 255 if d == 0 else 0
                            p.op('act', lambda e, c=c, d=d, HD=HD, col=col: e.copy(out=STO[:, c, :, d], in_=HD[:, :, col]),
                                 reads=(hk,), writes=('STO',))
                        lsrc = win[:, 2560 + c * 128:2560 + (c + 1) * 128].rearrange("(k p) c -> p k c", p=128)
                        lv, lk = ws.next(lsrc, 16, 128)
                        sl = new_slot()
                        pts = ps_tiles(sl)
                        def mmg(e, lv=lv, pts=pts):
                            ins = None
                            for kc in range(NCH):
                                for ti, (t0, tw) in enumerate(tok_tiles()):
                                    ins = e.matmul(pts[ti], lv[:, kc, :], H[:, kc, t0:t0 + tw], start=(kc == 0), stop=(kc == NCH - 1))
                            return ins
                        p.op('pe', mmg, reads=Hkeys + (lk,), writes=(('ps', sl),))
                        pa, pb = ps_AB(sl)
                        G1 = RB.rearrange("p s t -> p (s t)")
                        def g_sq(e, pa=pa, pb=pb):
                            e.activation(out=G1[:, 0:TA], in_=pa, func=AF.Square)
                            return e.activation(out=G1[:, TA:T], in_=pb, func=AF.Square)
                        p.op('act', g_sq, reads=(('ps', sl),), writes=('RB',))
                        def g_poly(e, pa=pa, pb=pb):
                            e.tensor_scalar(out=G1, in0=G1, scalar1=0.044715, scalar2=1.0, op0=ALU.mult, op1=ALU.add)
                            e.tensor_tensor(out=G1[:, 0:TA], in0=G1[:, 0:TA], in1=pa, op=ALU.mult)
                            return e.tensor_tensor(out=G1[:, TA:T], in0=G1[:, TA:T], in1=pb, op=ALU.mult)
                        p.op('dve', g_poly, reads=('RB', ('ps', sl)), writes=('RB',))
                        p.op('act', lambda e: e.activation(out=G1, in_=G1, func=AF.Sigmoid, scale=1.5957691216057308),
                             reads=('RB',), writes=('RB',))
                        RECB = ACTB[:, 7, :]
                        def g_fin(e, pa=pa, pb=pb):
                            e.tensor_tensor(out=G1[:, 0:TA], in0=G1[:, 0:TA], in1=pa, op=ALU.mult)
                            e.tensor_tensor(out=G1[:, TA:T], in0=G1[:, TA:T], in1=pb, op=ALU.mult)
                            e.tensor_tensor(out=TMP[:, 0, :], in0=TMP[:, 0, :], in1=TMP[:, 1, :], op=ALU.add)
                            return e.tensor_tensor(out=RECB, in0=G1, in1=TMP[:, 0, :], op=ALU.mult)
                        p.op('dve', g_fin, reads=('RB', ('ps', sl), 'HF', 'HB'), writes=('RB', 'HF', 'REC'))
                        wout_incr(dr["ab_w_out"][e_idx][1024 + c * 128:1024 + (c + 1) * 128, :], RECB, 'REC', 0)
                def sst(eng, sm):
                    eng.dma_start(out=dr["sto"], in_=STO[:].rearrange("p a b c -> p (a b c)")).then_inc(sm, 16)
                p.dma('sp', sst, 1, 'stst', reads=('STO',), is_output=True)

            def attention(e_idx):
                win = dr["ab_w_in"][e_idx]
                Hkeys = tuple(('H', c) for c in range(NCH))
                KT = ACTB[:, 0:2, :]
                KTC = ACTB[:, 2, 0:1024].rearrange("p (g s) -> p g s", g=2)
                VT = ACTB[:, 3:6, :].rearrange("p a t -> p (a t)")[:, 0:3584].rearrange("p (c d) -> p c d", c=14)
                COS, SIN = ACTB[:, 6, 0:TA], ACTB[:, 7, 0:TA]
                QTH = SQ[:, 0, :]
                QN = TMP[:, 0, :]
                ATT = TMP[:, 0, 0:640].bitcast(BF16)
                RDEN = TMP[:, 1, 0:TA]
                xpf = XP[:].rearrange("p s t -> p (s t)")[:, 0:1024].bitcast(BF16)
                PT = [xpf[:, 0:1024], xpf[:, 1024:2048]]
                SC = 128 ** -0.5
                if not dry:
                    def ldt(eng, sm):
                        eng.dma_start(out=ACTB[:, 6:8, 0:TA], in_=dr["cossin"]).then_inc(sm, 16)
                        eng.dma_start(out=KTC, in_=dr["ckT"]).then_inc(sm, 16)
                        eng.dma_start(out=VT[:, 0:4, :], in_=dr["cv"]).then_inc(sm, 16)
                    p.dma('pool', ldt, 3, 'ldatt', writes=('CS', 'KTC', 'VT'))
                vsrc = win[:, 1280:1536].rearrange("(k p) c -> p k c", p=128)
                vv, vk = ws.next(vsrc, 16, 256)
                if not dry:
                    for tt in range(T // 128):
                        bank = PSX if tt % 2 == 0 else PSY
                        bkey = 'psx' if tt % 2 == 0 else 'psy'
                        def mm(e, tt=tt, bank=bank):
                            ins = None
                            for kc in range(NCH):
                                ins = e.matmul(bank[:, 0:256], H[:, kc, tt * 128:(tt + 1) * 128], vv[:, kc, :],
                                               start=(kc == 0), stop=(kc == NCH - 1))
                            return ins
                        p.op('pe', mm, reads=Hkeys + (vk,), writes=(bkey,))
                        p.op('act', lambda e, bank=bank, tt=tt: e.copy(out=VT[:, 4 + tt, :], in_=bank[:, 0:256]),
                             reads=(bkey,), writes=('VT',))

                def qk_head(wv, wk, coff, gcol, dst, dkey):
                    s1 = new_slot()
                    pts = ps_tiles(s1)
                    def mm(e):
                        ins = None
                        for kc in range(NCH):
                            for ti, (t0, tw) in enumerate(tok_tiles()):
                                ins = e.matmul(pts[ti], wv[:, kc, coff:coff + 128], H[:, kc, t0:t0 + tw],
                                               start=(kc == 0), stop=(kc == NCH - 1))
                        return ins
                    p.op('pe', mm, reads=Hkeys + (wk,), writes=(('ps', s1),))
                    pa, pb = ps_AB(s1)
                    def sq(e):
                        e.activation(out=SQ[:, 0, 0:TA], in_=pa, func=AF.Square)
                        return e.activation(out=SQ[:, 0, TA:T], in_=pb, func=AF.Square)
                    p.op('act', sq, reads=(('ps', s1),), writes=(('SQ', 0),))
                    s2 = new_slot()
                    pts2 = ps_tiles(s2)
                    def mm2(e):
                        ins = None
                        for ti, (t0, tw) in enumerate(tok_tiles()):
                            ins = e.matmul(pts2[ti], ONES[:], SQ[:, 0, t0:t0 + tw], start=True, stop=True)
                        return ins
                    p.op('pe', mm2, reads=(('SQ', 0), 'ONES'), writes=(('ps', s2),))
                    pa2, pb2 = ps_AB(s2)
                    def ln(e):
                        e.activation(out=RSTD[:, 0:TA], in_=pa2, func=AF.Ln, scale=1.0 / 128, bias=1e-6)
                        e.activation(out=RSTD[:, TA:T], in_=pb2, func=AF.Ln, scale=1.0 / 128, bias=1e-6)
                        return e.activation(out=RSTD[:], in_=RSTD[:], func=AF.Exp, scale=-0.5)
                    p.op('act', ln, reads=(('ps', s2),), writes=('RSTD',))
                    def qn(e):
                        e.scalar_tensor_tensor(out=QN[:, 0:TA], in0=pa, scalar=QKN[:, gcol:gcol + 1], in1=RSTD[:, 0:TA],
                                               op0=ALU.mult, op1=ALU.mult)
                        return e.scalar_tensor_tensor(out=QN[:, TA:T], in0=pb, scalar=QKN[:, gcol:gcol + 1], in1=RSTD[:, TA:T],
                                                      op0=ALU.mult, op1=ALU.mult)
                    p.op('dve', qn, reads=(('ps', s1), 'RSTD', 'QKN'), writes=('QN',))
                    s3 = new_slot()
                    pts3 = ps_tiles(s3)
                    def mm3(e):
                        e.matmul(pts3[0], ROTM[:], QN[:, 0:512], start=True, stop=True)
                        return e.matmul(pts3[1], ROTM[:], QN[:, 512:1024], start=True, stop=True)
                    p.op('pe', mm3, reads=('QN', 'ROTM'), writes=(('ps', s3),))
                    pa3, _ = ps_AB(s3)
                    def rope(e):
                        e.tensor_tensor(out=TMP[:, 1, 0:TA], in0=QN[:, 0:TA], in1=COS, op=ALU.mult)
                        e.tensor_tensor(out=RSTD[:, 0:TA], in0=pa3, in1=SIN, op=ALU.mult)
                        return e.tensor_tensor(out=dst[:, 0:TA], in0=TMP[:, 1, 0:TA], in1=RSTD[:, 0:TA], op=ALU.add)
                    p.op('dve', rope, reads=('QN', 'CS', ('ps', s3)), writes=('T1', 'RSTD', dkey))
                    p.op('act', lambda e: e.copy(out=dst[:, TA:T], in_=QN[:, TA:T]), reads=('QN',), writes=(dkey,))

                ksrc = win[:, 1024:1280].rearrange("(k p) c -> p k c", p=128)
                kv_, kk = ws.next(ksrc, 16, 256)
                if not dry:
                    for g in range(2):
                        qk_head(kv_, kk, g * 128, 1, KT[:, g, :], 'KT')

                def attn_head(h):
                    g = h // 4
                    for sc in range(12):
                        par = sc % 2
                        skey = 'pS0' if par == 0 else 'pS1'
                        Sps = PS[:, 0:1024] if par == 0 else PS[:, 6 * 512:8 * 512]
                        kT = KTC[:, g, sc * 128:(sc + 1) * 128] if sc < 4 else KT[:, g, (sc - 4) * 128:(sc - 3) * 128]
                        def smm(e, Sps=Sps, kT=kT):
                            e.matmul(Sps[:, 0:512], kT, QTH[:, 0:512], start=True, stop=True)
                            return e.matmul(Sps[:, 512:1024], kT, QTH[:, 512:1024], start=True, stop=True)
                        p.op('pe', smm, reads=('QTH', 'KT', 'KTC'), writes=(skey,))
                        def ex(e, Sps=Sps, par=par, sc=sc):
                            ins = None
                            for qs in range(4):
                                ins = e.activation(out=PT[par][:, qs * 256:(qs + 1) * 256], in_=Sps[:, qs * 256:(qs + 1) * 256],
                                                   func=AF.Exp, scale=SC, bias=MASKB[:, sc * 4 + qs:sc * 4 + qs + 1])
                            return ins
                        p.op('act', ex, reads=(skey, 'MASKB'), writes=('PT%d' % par,))
                        def omm(e, par=par, sc=sc, g=g):
                            st, sp = (sc == 0), (sc == 11)
                            e.matmul(PS[:, 1024:1536], VT[:, sc, g * 128:(g + 1) * 128], PT[par][:, 0:512], start=st, stop=sp)
                            e.matmul(PS[:, 1536:2048], VT[:, sc, g * 128:(g + 1) * 128], PT[par][:, 512:1024], start=st, stop=sp)
                            e.matmul(PS[:, 2048:2560], ONES[:], PT[par][:, 0:512], start=st, stop=sp)
                            return e.matmul(PS[:, 2560:3072], ONES[:], PT[par][:, 512:1024], start=st, stop=sp)
                        p.op('pe', omm, reads=('PT%d' % par, 'VT', 'ONES'), writes=('pO', 'pD'))
                    def rd(e):
                        e.activation(out=RDEN, in_=PS[:, 2048:3072], func=AF.Ln)
                        return e.activation(out=RDEN, in_=RDEN, func=AF.Exp, scale=-1.0)
                    p.op('act', rd, reads=('pD',), writes=('RDEN',))
                    p.op('dve', lambda e: e.tensor_tensor(out=ATT[:, 0:TA], in0=PS[:, 1024:2048], in1=RDEN, op=ALU.mult),
                         reads=('pO', 'RDEN'), writes=('ATT',))
                    for j in range(2):
                        par = j % 2
                        skey = 'pS0' if par == 0 else 'pS1'
                        Sps = PS[:, 0:256] if par == 0 else PS[:, 6 * 512:6 * 512 + 256]
                        kT = KT[:, g, TA + j * 128:TA + (j + 1) * 128]
                        p.op('pe', lambda e, Sps=Sps, kT=kT: e.matmul(Sps, kT, QTH[:, TA:T], start=True, stop=True),
                             reads=('QTH', 'KT'), writes=(skey,))
                        p.op('act', lambda e, Sps=Sps, par=par: e.activation(out=PT[par][:, 0:256], in_=Sps, func=AF.Exp, scale=SC),
                             reads=(skey,), writes=('PT%d' % par,))
                        def omb(e, par=par, j=j, g=g):
                            e.matmul(PS[:, 1024:1280], VT[:, 12 + j, g * 128:(g + 1) * 128], PT[par][:, 0:256], start=(j == 0), stop=(j == 1))
                            return e.matmul(PS[:, 2048:2304], ONES[:], PT[par][:, 0:256], start=(j == 0), stop=(j == 1))
                        p.op('pe', omb, reads=('PT%d' % par, 'VT', 'ONES'), writes=('pO', 'pD'))
                    def rdb(e):
                        e.activation(out=RDEN[:, 0:TB], in_=PS[:, 2048:2304], func=AF.Ln)
                        return e.activation(out=RDEN[:, 0:TB], in_=RDEN[:, 0:TB], func=AF.Exp, scale=-1.0)
                    p.op('act', rdb, reads=('pD',), writes=('RDEN',))
                    p.op('dve', lambda e: e.tensor_tensor(out=ATT[:, TA:T], in0=PS[:, 1024:1280], in1=RDEN[:, 0:TB], op=ALU.mult),
                         reads=('pO', 'RDEN'), writes=('ATT',))

                for hp in range(4):
                    qsrc = win[:, hp * 256:(hp + 1) * 256].rearrange("(k p) c -> p k c", p=128)
                    qv, qk = ws.next(qsrc, 16, 256)
                    for sub in range(2):
                        h = 2 * hp + sub
                        if not dry:
                            qk_head(qv, qk, sub * 128, 0, QTH, 'QTH')
                            attn_head(h)
                        wout_incr(dr["ab_w_out"][e_idx][h * 128:(h + 1) * 128, :], ATT, 'ATT', 0)

            def mixer_even(l):
                norm_modulate(l, 1)
                kv_cache_out(l // 2)
                lru(l // 2)
                import os
                if not os.environ.get('NOATT'):
                    attention(l // 2)

            def mixer_odd(l):
                o_idx = l // 2
                norm_modulate(l, 1)
                win = dr["cd_w_in"][o_idx]
                Hkeys = tuple(('H', c) for c in range(NCH))
                Wsp = ACTB[:, 0:4, :].rearrange("p a t -> p (a t)").rearrange("p (c d) -> p c d", c=20)
                VB = ACTB[:, 4:6, :]
                X1B = ACTB[:, 6:8, :]
                CACC = TMP[:, 0, :].rearrange("p (s t) -> p s t", s=5)
                UT = TMP[:, 0, :].bitcast(BF16).rearrange("p (c d) -> p c d", c=10)
                FT = TMP[:, 1, :].bitcast(BF16).rearrange("p (c d) -> p c d", c=10)
                TT = TMP[:, 1, :]
                KFS, RSA, RSB, DECS, ES = (RSTD[:, i * 256:(i + 1) * 256] for i in range(5))
                ZF = SQ[:, 0, :]
                XP3 = XP[:, :, 0:258]
                ABSF = XP[:].rearrange("p s t -> p (s t)")[:, 0:128].bitcast(BF16)
                PSXB, PSYB = PSX.bitcast(BF16), PSY.bitcast(BF16)
                Wk = tuple(('ACTB', i) for i in range(4))
                VBk = (('ACTB', 4), ('ACTB', 5))
                X1k = (('ACTB', 6), ('ACTB', 7))

                if not dry:
                    p.op('dve', lambda e: e.memset(XP[:], 0.0), writes=('XP',))
                    Z0 = TMP[0:33, 1, :]
                    Z1T = TMP[0:64, 0, :]
                    def ldz(eng, sm):
                        eng.dma_start(out=Z0, in_=dr["z0T"]).then_inc(sm, 16)
                        eng.dma_start(out=HW1[:], in_=dr["hy_w1"][o_idx]).then_inc(sm, 16)
                        eng.dma_start(out=HW2[:], in_=dr["hy_w2"][o_idx]).then_inc(sm, 16)
                        eng.dma_start(out=HFB[:, 0:4], in_=dr["hyfb"]).then_inc(sm, 16)
                        eng.dma_start(out=NEGT[:], in_=dr["negt"]).then_inc(sm, 16)
                        eng.dma_start(out=HYC[:], in_=dr["hyc"]).then_inc(sm, 16)
                    p.dma('sp', ldz, 6, 'ldhy', writes=(('TMP', 1), 'HW', 'HFB', 'NEGT', 'HYC'))
                    p.op('dve', lambda e: e.tensor_scalar(out=HFB[:, 4:6], in0=HFB[:, 2:4], scalar1=1.0 / 3, scalar2=None, op0=ALU.mult),
                         reads=('HFB',), writes=('HFB',))
                    p.op('dve', lambda e: e.tensor_tensor(out=HFB[:, 6:8], in0=HFB[:, 4:6], in1=HFB[:, 0:2], op=ALU.mult),
                         reads=('HFB',), writes=('HFB',))
                    for layer in range(2):
                        src = Z0 if layer == 0 else Z1T
                        wl = HW1[0:33, :] if layer == 0 else HW2[0:64, :]
                        skey = ('TMP', 1) if layer == 0 else ('TMP', 0)
                        sl = new_slot()
                        pts = ps_tiles(sl)
                        def mmz(e, src=src, wl=wl, pts=pts):
                            ins = None
                            for ti, (t0, tw) in enumerate(tok_tiles()):
                                ins = e.matmul(pts[ti][0:64, :], wl, src[:, t0:t0 + tw], start=True, stop=True)
                            return ins
                        p.op('pe', mmz, reads=(skey, 'HW'), writes=(('ps', sl),))
                        S1 = RSTD[0:64, :]
                        pa, pb = ps_AB(sl)
                        def sn(e, pa=pa, pb=pb, layer=layer):
                            e.activation(out=S1[:, 0:TA], in_=pa[0:64, :], func=AF.Sin, scale=HFB[:, 4 + layer:5 + layer],
                                         bias=HFB[:, 6 + layer:7 + layer])
                            return e.activation(out=S1[:, TA:T], in_=pb[0:64, :], func=AF.Sin, scale=HFB[:, 4 + layer:5 + layer],
                                                bias=HFB[:, 6 + layer:7 + layer])
                        p.op('act', sn, reads=(('ps', sl), 'HFB'), writes=('RSTD',))
                        S2 = TMP[0:64, 1, :]
                        p.op('dve', lambda e: e.tensor_tensor(out=S2, in0=S1, in1=S1, op=ALU.mult), reads=('RSTD',), writes=(('TMP', 1),))
                        p.op('dve', lambda e: e.tensor_scalar(out=S2, in0=S2, scalar1=-4.0, scalar2=3.0, op0=ALU.mult, op1=ALU.add),
                             reads=(('TMP', 1),), writes=(('TMP', 1),))
                        dstz = Z1T if layer == 0 else Z2T[:]
                        dk = ('TMP', 0) if layer == 0 else 'Z2T'
                        p.op('dve', lambda e, dstz=dstz: e.tensor_tensor(out=dstz, in0=S2, in1=S1, op=ALU.mult),
                             reads=(('TMP', 1), 'RSTD'), writes=(dk,))

                def conv_chunk(col0, ci, dst, dkeys):
                    src = win[:, col0:col0 + 128].rearrange("(k p) c -> p k c", p=128)
                    wv, wk = ws.next(src, 16, 128)
                    if dry:
                        return
                    sl = new_slot()
                    pts = ps_tiles(sl)
                    def mm(e):
                        ins = None
                        for kc in range(NCH):
                            for ti, (t0, tw) in enumerate(tok_tiles()):
                                ins = e.matmul(pts[ti], wv[:, kc, :], H[:, kc, t0:t0 + tw], start=(kc == 0), stop=(kc == NCH - 1))
                        return ins
                    p.op('pe', mm, reads=Hkeys + (wk,), writes=(('ps', sl),))
                    pa, pb = ps_AB(sl)
                    def cp_in(e):
                        e.copy(out=XP3[:, 0:4, 1:257], in_=pa.rearrange("p (s t) -> p s t", s=4))
                        return e.copy(out=XP3[:, 4, 1:257], in_=pb)
                    p.op('act', cp_in, reads=(('ps', sl),), writes=('XP',))
                    def halo(e):
                        e.tensor_scalar(out=XP3[:, 1:4, 0:1], in0=XP3[:, 0:3, 256:257], scalar1=CONT[:, 0:1], scalar2=None, op0=ALU.mult)
                        return e.tensor_scalar(out=XP3[:, 0:3, 257:258], in0=XP3[:, 1:4, 1:2], scalar1=CONT[:, 0:1], scalar2=None, op0=ALU.mult)
                    p.op('dve', halo, reads=('XP', 'CONT'), writes=('XP',))
                    def conv(e):
                        e.tensor_scalar(out=CACC, in0=XP3[:, :, 0:256], scalar1=HYC[:, ci, 0:1], scalar2=HYC[:, ci, 3:4],
                                        op0=ALU.mult, op1=ALU.add)
                        e.scalar_tensor_tensor(out=CACC, in0=XP3[:, :, 1:257], scalar=HYC[:, ci, 1:2], in1=CACC, op0=ALU.mult, op1=ALU.add)
                        return e.scalar_tensor_tensor(out=dst, in0=XP3[:, :, 2:258], scalar=HYC[:, ci, 2:3], in1=CACC,
                                                      op0=ALU.mult, op1=ALU.add)
                    p.op('dve', conv, reads=('XP', 'HYC'), writes=(('TMP', 0),) + dkeys)

                def seg5(ap):
                    return ap.rearrange("p (s t) -> p s t", s=5)

                def build_ut(srcs, skeys):
                    for tc in range(T // 128):
                        bank, bkey = (PSXB, 'psx') if tc % 2 == 0 else (PSYB, 'psy')
                        def tr(e, tc=tc, bank=bank):
                            ins = None
                            for sub in range(2):
                                ins = e.transpose(bank[:, sub * 128:(sub + 1) * 128], srcs[sub][:, tc * 128:(tc + 1) * 128], IDB[:])
                            return ins
                        p.op('pe', tr, reads=skeys + ('IDB',), writes=(bkey,))
                        p.op('act', lambda e, tc=tc, bank=bank: e.copy(out=UT[:, tc, :], in_=bank[:, 0:256]),
                             reads=(bkey,), writes=(('TMP', 0),))

                def build_ft(o, cpi):
                    ch0 = o * 1024 + cpi * 256
                    def ldf(eng, sm):
                        eng.dma_start(out=W3S[:], in_=dr["hy_w3"][o_idx][:, ch0:ch0 + 256]).then_inc(sm, 16)
                        eng.dma_start(out=DECS, in_=dr["decb"][:, ch0:ch0 + 256]).then_inc(sm, 16)
                    p.dma('pool', ldf, 2, 'ldft', writes=('W3S', ('RS', 3)))
                    p.op('act', lambda e: e.activation(out=DECS, in_=DECS, func=AF.Abs),
                         reads=(('RS', 3),), writes=(('RS', 3),))
                    for nc_ in range(10):
                        zc0 = nc_ * 128
                        p.op('pe', lambda e, zc0=zc0: e.matmul(PSX[:, 0:256], Z2T[:, zc0:zc0 + 128], W3S[:], start=True, stop=True),
                             reads=('Z2T', 'W3S'), writes=('psx',))
                        p.op('act', lambda e, nc_=nc_: e.activation(out=ES, in_=DECS, func=AF.Exp, scale=NEGT[:, nc_:nc_ + 1]),
                             reads=(('RS', 3), 'NEGT'), writes=(('RS', 4),))
                        p.op('dve', lambda e, nc_=nc_: e.tensor_tensor(out=FT[:, nc_, :], in0=PSX[:, 0:256], in1=ES, op=ALU.mult),
                             reads=('psx', ('RS', 4)), writes=(('TMP', 1),))
                        p.op('act', lambda e, nc_=nc_: e.activation(out=ABSF, in_=FT[:, nc_, :], func=AF.Abs),
                             reads=(('TMP', 1),), writes=('XP',))
                        first = nc_ in (0, 8)
                        last = nc_ in (7, 9)
                        p.op('pe', lambda e, first=first, last=last: e.matmul(PSY[:, 0:256], ONES[:], ABSF, start=first, stop=last),
                             reads=('XP', 'ONES'), writes=('psy',))
                        if last:
                            RS = RSA if nc_ == 7 else RSB
                            rk = ('RS', 1) if nc_ == 7 else ('RS', 2)
                            def rinv(e, RS=RS):
                                e.activation(out=RS, in_=PSY[:, 0:256], func=AF.Ln)
                                return e.activation(out=RS, in_=RS, func=AF.Exp, scale=-1.0)
                            p.op('act', rinv, reads=('psy',), writes=(rk,))

                def forward(srckeys):
                    for blk in range(5):
                        if blk < 4:
                            fsrc = dr["ffA"][:, blk * 512:(blk + 1) * 512].rearrange("(k p) c -> p k c", p=128)
                            gsrc = dr["gtA"][:, blk * 512:(blk + 1) * 512].rearrange("(k p) c -> p k c", p=128)
                            nk, k0, RS, rk = 8, 0, RSA, ('RS', 1)
                        else:
                            fsrc = dr["ffB"].rearrange("(k p) c -> p k c", p=128)
                            gsrc = dr["gtB"].rearrange("(k p) c -> p k c", p=128)
                            nk, k0, RS, rk = 2, 8, RSB, ('RS', 2)
                        fv, fk = ws.next(fsrc, nk, 512)
                        gv, gk = ws.next(gsrc, nk, 512)
                        if dry:
                            continue
                        for ccl in range(4):
                            cc = blk * 4 + ccl
                            bu, bg_ = [(6, 7), (0, 1), (2, 3), (4, 5)][cc % 4]
                            PU, PG = PS[:, bu * 512:bu * 512 + 256], PS[:, bg_ * 512:bg_ * 512 + 256]
                            ku, kg = ('bank', bu), ('bank', bg_)
                            def fmm(e, fv=fv, ccl=ccl, nk=nk, k0=k0, PU=PU):
                                ins = None
                                for k in range(nk):
                                    ins = e.matmul(PU, fv[:, k, ccl * 128:(ccl + 1) * 128], UT[:, k0 + k, :],
                                                   start=(k == 0), stop=(k == nk - 1))
                                return ins
                            p.op('pe', fmm, reads=(fk, ('TMP', 0)), writes=(ku,))
                            def gmm(e, gv=gv, ccl=ccl, nk=nk, k0=k0, PG=PG):
                                ins = None
                                for k in range(nk):
                                    ins = e.matmul(PG, gv[:, k, ccl * 128:(ccl + 1) * 128], FT[:, k0 + k, :],
                                                   start=(k == 0), stop=(k == nk - 1))
                                return ins
                            p.op('pe', gmm, reads=(gk, ('TMP', 1)), writes=(kg,))
                            KF2 = KFS if cc % 2 == 0 else ES
                            kk2 = ('RS', 0) if cc % 2 == 0 else ('RS', 4)
                            p.op('dve', lambda e, RS=RS, PG=PG, KF2=KF2: e.tensor_tensor(out=KF2, in0=PG, in1=RS, op=ALU.mult),
                                 reads=(kg, rk), writes=(kk2,))
                            p.op('dve', lambda e, cc=cc, PU=PU, KF2=KF2: e.tensor_tensor(out=Wsp[:, cc, :], in0=PU, in1=KF2, op=ALU.mult),
                                 reads=(ku, kk2), writes=Wk)

                def inverse():
                    stages = []
                    for th in range(2):
                        for half in range(2):
                            src = dr["fiA"][half * 1024:(half + 1) * 1024, th * 512:(th + 1) * 512].rearrange("(k p) c -> p k c", p=128)
                            stages.append((th, half, src))
                    first = True
                    for th, half, src in stages:
                        iv, ik = ws.next(src, 8, 512)
                        if dry:
                            continue
                        def imm(e, iv=iv, th=th, half=half):
                            ins = None
                            for sub in range(2):
                                pt = ps_tiles(sub)[th]
                                for ccl in range(8):
                                    cc = half * 8 + ccl
                                    ins = e.matmul(pt, Wsp[:, cc, sub * 128:(sub + 1) * 128], iv[:, ccl, :],
                                                   start=(cc == 0), stop=(cc == 15))
                            return ins
                        p.op('pe', imm, reads=Wk + (ik,), writes=(('ps', 0), ('ps', 1)))
                    bv, bk = ws.next(dr["fiB"].rearrange("(k p) c -> p k c", p=128), 4, 256)
                    if dry:
                        return
                    def imb(e):
                        ins = None
                        for sub in range(2):
                            pt = ps_tiles(sub)[2]
                            for ccl in range(4):
                                ins = e.matmul(pt, Wsp[:, 16 + ccl, sub * 128:(sub + 1) * 128], bv[:, ccl, :],
                                               start=(ccl == 0), stop=(ccl == 3))
                        return ins
                    p.op('pe', imb, reads=Wk + (bk,), writes=(('ps', 0), ('ps', 1)))

                import os
                for cpi in range(0 if not os.environ.get('NOHY') else 4, 4):
                    for sub in range(2):
                        c = 2 * cpi + sub
                        conv_chunk(2048 + c * 128, 16 + c, seg5(VB[:, sub, :]) if not dry else None, (('ACTB', 4 + sub),))
                        conv_chunk(c * 128, c, seg5(X1B[:, sub, :]) if not dry else None, (('ACTB', 6 + sub),))
                    if not dry:
                        build_ut([VB[:, 0, :], VB[:, 1, :]], VBk)
                        build_ft(0, cpi)
                    forward(VBk)
                    inverse()
                    if not dry:
                        slotc[0] = 0
                        for sub in range(2):
                            c = 2 * cpi + sub
                            pa, pb = ps_AB(sub)
                            def z1a(e, sub=sub, c=c, pa=pa, pb=pb):
                                e.scalar_tensor_tensor(out=TT[:, 0:TA], in0=VB[:, sub, 0:TA], scalar=HSK[:, 0, c:c + 1], in1=pa,
                                                       op0=ALU.mult, op1=ALU.add)
                                return e.scalar_tensor_tensor(out=TT[:, TA:T], in0=VB[:, sub, TA:T], scalar=HSK[:, 0, c:c + 1], in1=pb,
                                                              op0=ALU.mult, op1=ALU.add)
                            p.op('dve', z1a, reads=(('ACTB', 4 + sub), ('ps', sub), 'HSK'), writes=(('TMP', 1),))
                            p.op('dve', lambda e, sub=sub: e.tensor_tensor(out=X1B[:, sub, :], in0=TT, in1=X1B[:, sub, :], op=ALU.mult),
                                 reads=(('TMP', 1), ('ACTB', 6 + sub)), writes=(('ACTB', 6 + sub),))
                    for sub in range(2):
                        c = 2 * cpi + sub
                        conv_chunk(1024 + c * 128, 8 + c, seg5(VB[:, sub, :]) if not dry else None, (('ACTB', 4 + sub),))
                    if not dry:
                        build_ut([X1B[:, 0, :], X1B[:, 1, :]], X1k)
                        build_ft(1, cpi)
                    forward(X1k)
                    inverse()
                    if not dry:
                        slotc[0] = 0
                    if not dry:
                        for sub in range(2):
                            c = 2 * cpi + sub
                            pa, pb = ps_AB(sub)
                            def z2a(e, sub=sub, c=c, pa=pa, pb=pb):
                                e.scalar_tensor_tensor(out=TT[:, 0:TA], in0=X1B[:, sub, 0:TA], scalar=HSK[:, 1, c:c + 1], in1=pa,
                                                       op0=ALU.mult, op1=ALU.add)
                                return e.scalar_tensor_tensor(out=TT[:, TA:T], in0=X1B[:, sub, TA:T], scalar=HSK[:, 1, c:c + 1], in1=pb,
                                                              op0=ALU.mult, op1=ALU.add)
                            p.op('dve', z2a, reads=(('ACTB', 6 + sub), ('ps', sub), 'HSK'), writes=(('TMP', 1),))
                            p.op('dve', lambda e, sub=sub: e.tensor_tensor(out=X1B[:, sub, :], in0=TT, in1=VB[:, sub, :], op=ALU.mult),
                                 reads=(('TMP', 1), ('ACTB', 4 + sub)), writes=(('ACTB', 6 + sub),))
                    wout_incr(dr["cd_w_out"][o_idx][cpi * 256:(cpi + 1) * 256, :],
                              [X1B[:, 0, :], X1B[:, 1, :]] if not dry else [None, None], [('ACTB', 6), ('ACTB', 7)], l)


                PLT = TMP[:, 0, :].bitcast(BF16).rearrange("p (c d) -> p c d", c=10)
                DMB = TMP[:, 1, :].bitcast(BF16).rearrange("p (c d) -> p c d", c=2)
                POOLED = SQ[:, 0, :]
                POOLED1 = XP[:].rearrange("p s t -> p (s t)")[:, 0:640].bitcast(BF16)
                p.sertags = ('mmd',)
                for g in range(0 if not os.environ.get('NOPOOL') else 4, 4):
                    psrc = win[:, 3072 + g * 256:3072 + (g + 1) * 256].rearrange("(k p) c -> p k c", p=128)
                    pv, pk = ws.next(psrc, 16, 256)
                    if not dry:
                        for tc in range(T // 128):
                            bank, bkey = (PSX, 'psx') if tc % 2 == 0 else (PSY, 'psy')
                            def mmp(e, tc=tc, bank=bank, pv=pv):
                                ins = None
                                for kc in range(NCH):
                                    ins = e.matmul(bank[:, 0:256], H[:, kc, tc * 128:(tc + 1) * 128], pv[:, kc, :],
                                                   start=(kc == 0), stop=(kc == NCH - 1))
                                return ins
                            p.op('pe', mmp, reads=Hkeys + (pk,), writes=(bkey,), tag='mmp')
                            p.op('act', lambda e, tc=tc, bank=bank: e.copy(out=PLT[:, tc, :], in_=bank[:, 0:256]),
                                 reads=(bkey,), writes=(('TMP', 0),), tag='plt')
                    dv, dk = ws.next(dr["poolD"][g], 30, 128)
                    if not dry:
                        for cl in range(2):
                            sl = new_slot()
                            pts = ps_tiles(sl)
                            def mmd(e, cl=cl, pts=pts, dv=dv):
                                ins = None
                                for tcn in range(10):
                                    dstp = pts[tcn // 4][:, (tcn % 4) * 128:(tcn % 4 + 1) * 128]
                                    offs = [o for o in (-1, 0, 1) if 0 <= tcn + o < 10]
                                    for oi, o in enumerate(offs):
                                        st = (oi == 0) and (tcn in (0, 4, 8))
                                        sp = (oi == len(offs) - 1) and (tcn in (3, 7, 9))
                                        ins = e.matmul(dstp, PLT[:, tcn + o, cl * 128:(cl + 1) * 128], dv[:, tcn * 3 + o + 1, :],
                                                       start=st, stop=sp, skip_group_check=True)
                                return ins
                            p.op('pe', mmd, reads=(('TMP', 0), dk), writes=(('ps', sl),), tag='mmd')
                            pa, pb = ps_AB(sl)
                            def cpd(e, cl=cl, pa=pa, pb=pb):
                                e.copy(out=DMB[:, cl, 0:TA], in_=pa)
                                return e.copy(out=DMB[:, cl, TA:T], in_=pb)
                            p.op('act', cpd, reads=(('ps', sl),), writes=(('TMP', 1),), tag='cpd')
                    wv_, wk_ = ws.next(dr["pool_w"][o_idx][g].rearrange("(k p) c -> p k c", p=128), 2, 256)
                    for j in range(2):
                        c = 2 * g + j
                        if not dry:
                            sl = new_slot()
                            pts = ps_tiles(sl)
                            def mmw(e, j=j, pts=pts, wv_=wv_):
                                ins = None
                                for i in range(2):
                                    for ti, (t0, tw) in enumerate(tok_tiles()):
                                        ins = e.matmul(pts[ti], wv_[:, i, j * 128:(j + 1) * 128], DMB[:, i, t0:t0 + tw],
                                                       start=(i == 0), stop=(i == 1))
                                return ins
                            p.op('pe', mmw, reads=(('TMP', 1), wk_), writes=(('ps', sl),), tag='mmw')
                            pa, pb = ps_AB(sl)
                            PD = POOLED if j == 0 else POOLED1
                            def psc(e, c=c, pa=pa, pb=pb, PD=PD):
                                e.activation(out=PD[:, 0:TA], in_=pa, func=AF.Identity, scale=PSCL[:, c:c + 1])
                                return e.activation(out=PD[:, TA:T], in_=pb, func=AF.Identity, scale=PSCL[:, c:c + 1])
                            p.op('act', psc, reads=(('ps', sl), 'PSCL'), writes=((('SQ', 0),) if j == 0 else ('XP',)), tag='psc')
                    wout_incr(dr["cd_w_out"][o_idx][1024 + g * 256:1024 + (g + 1) * 256, :],
                              [POOLED, POOLED1] if not dry else [None, None], [('SQ', 0), 'XP'], l)

            modulation(0, 0, 3)
            mod_derive(0, 0)
            if stop >= 1:
                ffn(0, 0, 0, bg=mod_tasks(0, 3, 9))
            else:
                modulation(0, 3, 9)
            mod_derive(0, 1)
            mod_derive(0, 2)
            if stop >= 2:
                mixer_even(0)
            if stop >= 3:
                ffn(0, 2, 1, bg=mod_tasks(1, 0, 9))
                for j in range(3):
                    mod_derive(1, j)
            if stop >= 4:
                ffn(1, 0, 0)
            if stop >= 5:
                mixer_odd(1)
            if stop >= 6:
                ffn(1, 2, 1)

            yv = dr["yT"].rearrange("(c p) t -> p c t", p=128)
            for q in range(4):
                def fn(eng, sm, q=q):
                    eng.dma_start(out=yv[:, 4 * q:4 * q + 4, :], in_=X[:, 4 * q:4 * q + 4, :]).then_inc(sm, 16)
                p.dma('sp', fn, 1, 'sty', reads=tuple(('Xc', c) for c in range(4 * q, 4 * q + 4)), is_output=True)
            def fnm(eng, sm):
                eng.dma_start(out=dr["modout"], in_=MOD[:].rearrange("p a b c d -> p (a b c d)")).then_inc(sm, 16)
            p.dma('sp', fnm, 1, 'stm', reads=tuple(('MOD', 0, q) for q in range(9)), is_output=True)

            if dry:
                plan = ws.rec
            else:
                p.emit()
    return nc


def core_tokens(inputs, core):
    xp, xs = inputs['x_prompt'], inputs['x_sample']
    if core < 2:
        return np.concatenate([xs[core], xp[core]], axis=0), True
    segs = [xp[2 + (core - 2) * 5 + j] for j in range(5)]
    return np.concatenate(segs, axis=0), False


def make_in_maps(inputs, stop=99):
    f32 = np.float32
    normg = np.ascontiguousarray(inputs['norm_g'].reshape(6, NCH, 128).transpose(2, 0, 1)).astype(f32)
    bmod = np.ascontiguousarray(inputs['b_mod'].reshape(2, 9 * NCH, 128).transpose(2, 0, 1)).astype(f32)
    qkn = np.stack([inputs['ab_q_norm'][0], inputs['ab_k_norm'][0]], axis=1).astype(f32)
    rotm = np.zeros((128, 128), f32)
    for i in range(64):
        rotm[2 * i + 1, 2 * i] = -1.0
        rotm[2 * i, 2 * i + 1] = 1.0
    tt = np.arange(TA)
    rr, cc = (tt // 64).astype(np.float64), (tt % 64).astype(np.float64)
    inv = 10000.0 ** (-np.arange(0, 64, 2, dtype=np.float64) / 64)
    ang = np.concatenate([rr[:, None] * inv, cc[:, None] * inv], axis=-1)
    cossin_long = np.stack([np.repeat(np.cos(ang), 2, axis=1).T, np.repeat(np.sin(ang), 2, axis=1).T], axis=1).astype(f32)
    cossin_id = np.stack([np.ones((128, TA)), np.zeros((128, TA))], axis=1).astype(f32)
    maskb_long = np.zeros((128, 48), f32)
    maskb_prompt = np.full((128, 48), -30000.0, f32)
    for sc in range(4, 12):
        maskb_prompt[:, sc * 4 + (sc - 4) // 2] = 0.0
    bf = ml_dtypes.bfloat16
    def dft_consts(L):
        j = np.arange(2 * L)
        k = np.where(j <= L, j, j - L)
        is_sin = j > L
        sidx = np.arange(L)
        arg = np.pi * np.outer(k, sidx) / L
        Cm = np.where(is_sin[:, None], np.sin(arg), np.cos(arg))
        w = np.where((k == 0) | (k == L), 1.0, 2.0) / (2 * L)
        nf = np.where(sidx == 0, 1.0, 2.0)
        G = w[:, None] * nf[None, :] * np.cos(arg)
        return Cm.T.copy(), Cm.copy(), G.T.copy()
    def z0feat(L):
        n = np.arange(L, dtype=np.float64)
        t = n / max(L - 1, 1)
        bands = np.linspace(1e-4, 15.0, 16)
        f = 2.0 * np.pi * n[:, None] * bands[None, :] / L
        return np.concatenate([t[:, None], np.cos(f), -np.sin(f)], axis=-1), t
    FF1k, FI1k, GT1k = dft_consts(1024)
    FF256, FI256, GT256 = dft_consts(256)
    ffA_long, fiA_long, gtA_long = FF1k.astype(bf), FI1k.astype(bf), GT1k.astype(bf)
    ffA_p = np.zeros((1024, 2048)); fiA_p = np.zeros((2048, 1024)); gtA_p = np.zeros((1024, 2048))
    for sgi in range(4):
        ffA_p[sgi * 256:(sgi + 1) * 256, sgi * 512:(sgi + 1) * 512] = FF256
        fiA_p[sgi * 512:(sgi + 1) * 512, sgi * 256:(sgi + 1) * 256] = FI256
        gtA_p[0:256, sgi * 512:(sgi + 1) * 512] = GT256
    ffA_prompt, fiA_prompt, gtA_prompt = ffA_p.astype(bf), fiA_p.astype(bf), gtA_p.astype(bf)
    ffB, fiB, gtB = FF256.astype(bf), FI256.astype(bf), GT256.astype(bf)
    z1k, t1k = z0feat(1024)
    z256, t256 = z0feat(256)
    z0T_long = np.concatenate([z1k.T, z256.T], axis=1).astype(f32)
    z0T_prompt = np.concatenate([z256.T, np.zeros((33, 768)), z256.T], axis=1).astype(f32)
    negt_long = np.concatenate([-t1k.reshape(8, 128).T, -t256.reshape(2, 128).T], axis=1).astype(f32)
    negt_prompt = np.concatenate([-t256.reshape(2, 128).T, np.full((128, 6), -1.0e4), -t256.reshape(2, 128).T], axis=1).astype(f32)
    hyfb = np.stack([inputs['hy_b1'][0], inputs['hy_b2'][0], inputs['hy_freq'][0, 0], inputs['hy_freq'][0, 1]], axis=1).astype(f32)
    hyc = np.zeros((128, 24, 4), f32)
    hyc[:, :, 0:3] = inputs['hy_conv_w'][0].reshape(3, 24, 128).transpose(2, 1, 0)
    hyc[:, :, 3] = inputs['hy_conv_b'][0].reshape(24, 128).T
    hsk = np.ascontiguousarray(inputs['hy_skip'][0].reshape(2, 8, 128).transpose(2, 0, 1)).astype(f32)
    decb = np.ascontiguousarray(np.broadcast_to(inputs['hy_decay'][0][None, :], (128, 2048))).astype(f32)
    def pool_blocks(bounds):
        out = np.zeros((4, 128, 30, 128))
        for gi, w in enumerate((2, 4, 8, 16)):
            DT = np.zeros((T, T))
            for (a, b) in bounds:
                L = b - a
                for t in range(L):
                    lo, hi = max(t - w // 2, 0), min(t + w // 2, L)
                    DT[a + lo:a + hi, a + t] += 1.0 / (hi - lo)
                    DT[a + t, a + t] -= 1.0
            for tcn in range(10):
                for o in (-1, 0, 1):
                    if 0 <= tcn + o < 10:
                        out[gi, :, tcn * 3 + o + 1, :] = DT[(tcn + o) * 128:(tcn + o + 1) * 128, tcn * 128:(tcn + 1) * 128]
        return out.astype(bf)
    poolD_long = pool_blocks([(0, 1024), (1024, 1280)])
    poolD_prompt = pool_blocks([(i * 256, (i + 1) * 256) for i in range(5)])
    pscl = np.ascontiguousarray(inputs['pool_scale'][0].reshape(8, 128).T).astype(f32)
    gw_host = np.ascontiguousarray(
        inputs['lru_gate_w'][0].reshape(4, 8, 128, 128).transpose(1, 2, 0, 3)).astype(f32)
    lruc = np.zeros((128, 8, 10), f32)
    lruc[:, :, 0:4] = inputs['lru_conv_w'][0].reshape(4, 8, 128).transpose(2, 1, 0)
    lruc[:, :, 4] = inputs['lru_conv_b'][0].reshape(8, 128).T
    lruc[:, :, 5:9] = inputs['lru_gate_b'][0].reshape(4, 8, 128).transpose(2, 1, 0)
    lrul = np.ascontiguousarray(inputs['lru_lambda'][0].reshape(2, 8, 128).transpose(2, 0, 1)).astype(f32)
    maps = []
    for core in range(NCORES):
        xt, is_long = core_tokens(inputs, core)
        condA = inputs['c'][core] if is_long else inputs['c_ctx']
        condB = inputs['c_ctx']
        cond = np.stack([condA.reshape(NCH, 128).T, condB.reshape(NCH, 128).T], axis=1)
        m = {
            "xT": np.ascontiguousarray(xt.T),
            "cond": np.ascontiguousarray(cond).astype(f32),
            "normg": normg,
            "bmod": bmod,
            "w_mod": inputs['w_mod'],
            "ffn_w13": inputs['ffn_w13'],
            "ffn_w2": inputs['ffn_w2'],
            "ab_w_in": inputs['ab_w_in'],
            "knb": np.ascontiguousarray(np.broadcast_to(inputs['ab_k_norm'][0][None, :], (128, 128))).astype(f32),
            "lru_gate_w": gw_host,
            "ab_w_out": inputs['ab_w_out'],
            "qkn": qkn, "rotm": rotm,
            "cossin": cossin_long if is_long else cossin_id,
            "ckT": (np.ascontiguousarray(inputs['cache_k'][core, 0].transpose(2, 1, 0)).astype(f32)
                    if is_long else np.zeros((128, 2, 512), f32)),
            "cv": (np.ascontiguousarray(inputs['cache_v'][core, 0].reshape(4, 128, 256).transpose(1, 0, 2)).astype(f32)
                   if is_long else np.zeros((128, 4, 256), f32)),
            "maskb": maskb_long if is_long else maskb_prompt,
            "cd_w_in": inputs['cd_w_in'], "cd_w_out": inputs['cd_w_out'],
            "hy_w1": inputs['hy_w1'], "hy_w2": inputs['hy_w2'], "hy_w3": inputs['hy_w3'],
            "hyfb": hyfb, "hyc": hyc, "hsk": hsk, "decb": decb,
            "z0T": z0T_long if is_long else z0T_prompt,
            "negt": negt_long if is_long else negt_prompt,
            "ffA": ffA_long if is_long else ffA_prompt,
            "gtA": gtA_long if is_long else gtA_prompt,
            "fiA": fiA_long if is_long else fiA_prompt,
            "ffB": ffB, "gtB": gtB, "fiB": fiB,
            "poolD": poolD_long if is_long else poolD_prompt,
            "pool_w": inputs['pool_w'], "pscl": pscl,
            "lruc": lruc, "lrul": lrul,
            "lruh": (np.stack([inputs['state_lru_fwd'][core, 0].reshape(8, 128).T,
                               inputs['state_lru_bwd'][core, 0].reshape(8, 128).T], axis=1).astype(f32)
                     if is_long else np.zeros((128, 2, 8), f32)),
            "cont": np.full((128, 1), 1.0 if is_long else 0.0, f32),
        }
        maps.append(m)
    return maps


def kernel(**inputs):
    inputs = {k: np.asarray(v) for k, v in inputs.items()}
    nc = build()
    maps = make_in_maps(inputs)
    res = run_bass_kernel_spmd(nc, maps, core_ids=list(range(NCORES)))
    f32 = np.float32
    y_prompt = np.zeros((32, 256, D), f32)
    y_sample = np.zeros((2, 1024, D), f32)
    nk = np.zeros((32, 1, 256, 2, 128), f32)
    nv = np.zeros((32, 1, 256, 2, 128), f32)
    sf = np.zeros((32, 1, 1024), f32)
    sb = np.zeros((32, 1, 1024), f32)
    for core in range(NCORES):
        r = res.results[core]
        y = np.ascontiguousarray(r["yT"].T)
        kvo = r["kvo"]
        sto = r["sto"].reshape(128, 8, 5, 2)
        if core < 2:
            y_sample[core] = y[:1024]
            segs = [(4, core)]
        else:
            segs = [(j, 2 + (core - 2) * 5 + j) for j in range(5)]
        for j, b in segs:
            y_prompt[b] = y[j * 256:(j + 1) * 256]
            nk[b, 0] = kvo[j * 256:(j + 1) * 256, 0:256].reshape(256, 2, 128)
            nv[b, 0] = kvo[j * 256:(j + 1) * 256, 256:512].reshape(256, 2, 128)
            sf[b, 0] = sto[:, :, j, 0].T.reshape(1024)
            sb[b, 0] = sto[:, :, j, 1].T.reshape(1024)
    return (y_prompt, y_sample, nk, nv, sf, sb)
```

```python
import contextlib
import numpy as np
import ml_dtypes
import concourse.bass as bass
import concourse.mybir as mybir
from concourse.bass_utils import run_bass_kernel_spmd

F32 = mybir.dt.float32
BF16 = mybir.dt.bfloat16
AF = mybir.ActivationFunctionType
ALU = mybir.AluOpType

NCORES = 8
D = 2048
NCH = 16
T = 1280
TA = 1024
TB = 256
DFF = 5632
NHC = 44
GRP = 8
NST = 4
HOLD = 3
STAGE = 4096
ENGS = ['pe', 'act', 'dve', 'pool', 'sp']


class Prog:
    def __init__(self, nc, es, dry=False):
        self.nc, self.es, self.dry = nc, es, dry
        self.streams = {e: [] for e in ENGS}
        self.cnt = {e: 0 for e in ENGS}
        self.sem = {}
        self.dcnt = {}
        self.lastw = {}
        self.rd = {}
        self.waited = {e: {} for e in ENGS}
        self.out_tokens = []
        B = lambda *b: tuple(('bank', i) for i in b)
        self.alias = {('ps', 0): B(0, 1, 2), ('ps', 1): B(3, 4, 5), 'psx': B(6), 'psy': B(7),
                      'pS': B(0, 1), 'pO': B(2, 3), 'pD': B(4, 5),
                      'XC': (('ACTB', 0), ('ACTB', 1)), 'RB': (('ACTB', 2), ('ACTB', 3)),
                      'IB': (('ACTB', 4), ('ACTB', 5)), 'XCB': (('ACTB', 6),),
                      ('KVO', 0, 'k'): (('ACTB', 7), 'kvok'), ('KVO', 0, 'v'): (('ACTB', 7), 'kvov'),
                      'AB': ('RSTD',), 'HF': (('TMP', 0),), 'HB': (('TMP', 1),), 'REC': (('ACTB', 7),),
                      'XP': (('XPk', 0), ('XPk', 1)), 'PT0': (('XPk', 0),), 'PT1': (('XPk', 1),), 'ATT': (('TMP', 0),), 'RDEN': (('TMP', 1),),
                      'QN': (('TMP', 0),), 'T1': (('TMP', 1),), 'QTH': (('SQ', 0),),
                      'KT': (('ACTB', 0), ('ACTB', 1)), 'KTC': (('ACTB', 2),), 'VT': (('ACTB', 3), ('ACTB', 4), ('ACTB', 5)),
                      'CS': (('ACTB', 6), ('ACTB', 7)),
                      'pS0': B(0, 1), 'pS1': B(6, 7), ('MROW', 0): ('Z2T',), 'JUNK': (('TMP', 1),),
                      'RSTD': tuple(('RS', i) for i in range(5))}
        if not dry:
            for e in ['pe', 'act', 'dve', 'pool']:
                self.sem[e] = es.enter_context(nc.semaphore('s_' + e))

    def _exp(self, keys):
        out = []
        for k in keys:
            if k in self.alias:
                out.extend(self.alias[k])
            else:
                out.append(k)
        return tuple(out)

    serialize = False

    def _collect(self, eng, reads, writes):
        need = {}
        if self.serialize:
            for e2 in ('pe', 'act', 'dve', 'pool'):
                if self.cnt[e2] and not (e2 == 'pe' and eng == 'pe'):
                    need[e2] = self.cnt[e2]
            for k2, v2 in self.dcnt.items():
                need[k2] = v2
        def add(tok):
            if tok is None:
                return
            s, v = tok
            if s == 'pe' and eng == 'pe':
                return
            if need.get(s, 0) < v:
                need[s] = v
        for k in reads:
            add(self.lastw.get(k))
        for k in writes:
            add(self.lastw.get(k))
            for s, v in self.rd.get(k, {}).items():
                add((s, v))
        waits = []
        for s, v in need.items():
            if self.waited[eng].get(s, 0) >= v:
                continue
            self.waited[eng][s] = v
            waits.append((s, v))
        return waits

    def _commit(self, tok, reads, writes):
        s, v = tok
        for k in reads:
            d = self.rd.setdefault(k, {})
            if d.get(s, 0) < v:
                d[s] = v
        for k in writes:
            self.lastw[k] = tok
            self.rd[k] = {}

    sertags = ()

    def op(self, eng, fn, reads=(), writes=(), tag=None):
        if self.dry:
            return None
        self.serialize = tag is not None and tag in self.sertags
        reads, writes = self._exp(reads), self._exp(writes)
        waits = self._collect(eng, reads, writes)
        self.cnt[eng] += 1
        tok = (eng, self.cnt[eng])
        self.streams[eng].append((waits, fn, eng))
        self._commit(tok, reads, writes)
        return tok

    def dma(self, queue, fn, ndma, semname, reads=(), writes=(), is_output=False, tag=None):
        if self.dry:
            return None
        self.serialize = tag is not None and tag in self.sertags
        reads, writes = self._exp(reads), self._exp(writes)
        key = 'd:' + semname
        if key not in self.sem:
            self.sem[key] = self.es.enter_context(self.nc.semaphore('d_' + semname))
            self.dcnt[key] = 0
        waits = self._collect(queue, reads, writes)
        self.dcnt[key] += 16 * ndma
        tok = (key, self.dcnt[key])
        self.streams[queue].append((waits, fn, key))
        self._commit(tok, reads, writes)
        if is_output:
            self.out_tokens.append(tok)
        return tok

    def emit(self):
        nc = self.nc
        fin = {}
        for s, v in self.out_tokens:
            fin[s] = max(fin.get(s, 0), v)
        streams, sem = self.streams, self.sem

        def run(engname):
            def body(eng):
                for waits, fn, kind in streams[engname]:
                    for s, v in waits:
                        eng.wait_ge(sem[s], v)
                    if kind in ('pe', 'act', 'dve', 'pool'):
                        ins = fn(eng)
                        ins.then_inc(sem[kind], 1)
                    else:
                        fn(eng, sem[kind])
                if engname == 'sp':
                    for s, v in fin.items():
                        eng.wait_ge(sem[s], v)
            return body

        with nc.Block() as block:
            block.sync(run('sp'))
            block.tensor(run('pe'))
            block.scalar(run('act'))
            block.vector(run('dve'))
            block.gpsimd(run('pool'))


class WStream:
    def __init__(self, prog, wbuf, plan=None):
        self.p, self.wbuf = prog, wbuf
        self.plan = plan
        self.rec = []
        self.n = 0
        self.emitted = 0

    def _emit_dma(self, i):
        src, a, b = self.plan[i]
        slot = i % NST
        dst = self._view(slot, a[0] * a[1] if isinstance(a, tuple) else a, b)
        def fn(eng, sem, dst=dst, src=src):
            eng.dma_start(out=dst, in_=src).then_inc(sem, 16)
        self.p.dma('pool', fn, 1, 'w%d' % slot, reads=(), writes=(('w', slot),))

    def _view(self, slot, a, b):
        if isinstance(a, tuple):
            a0, a1 = a
            return self.wbuf[:, slot, 0:a0 * a1 * b].rearrange("p (a c b) -> p a c b", a=a0, c=a1)
        return self.wbuf[:, slot, 0:a * b].rearrange("p (a b) -> p a b", a=a)

    def next(self, src, a, b):
        i = self.n
        self.n += 1
        if self.p.dry:
            self.rec.append((src, a, b))
            return None, None
        while self.emitted < min(len(self.plan), i + NST - HOLD + 1):
            self._emit_dma(self.emitted)
            self.emitted += 1
        slot = i % NST
        return self._view(slot, a, b), ('w', slot)


class K:
    pass


def tok_tiles():
    return [(0, 512), (512, 512), (1024, 256)]


def build(stop=6):
    nc = bass.Bass("TRN2", target_bir_lowering=False)
    dr = {}

    def din(name, shape, dt=F32):
        dr[name] = nc.dram_tensor(name, list(shape), dt, kind="ExternalInput").ap()
        return dr[name]

    def dout(name, shape, dt=F32):
        dr[name] = nc.dram_tensor(name, list(shape), dt, kind="ExternalOutput").ap()
        return dr[name]

    din("xT", [D, T])
    din("cond", [128, 2, NCH])
    din("normg", [128, 6, NCH])
    din("bmod", [128, 2, 9 * NCH])
    din("w_mod", [2, D, 9 * D])
    din("ffn_w13", [2, 2, D, 2 * DFF])
    din("ffn_w2", [2, 2, DFF, D])
    din("ab_w_in", [1, D, 3584])
    din("knb", [128, 128])
    din("lru_gate_w", [8, 128, 4, 128])
    din("lruc", [128, 8, 10])
    din("lrul", [128, 2, 8])
    din("lruh", [128, 2, 8])
    din("cont", [128, 1])
    dout("sto", [128, 8 * 5 * 2])
    din("ab_w_out", [1, D, D])
    din("qkn", [128, 2])
    din("rotm", [128, 128])
    din("cossin", [128, 2, TA])
    din("ckT", [128, 2, 512])
    din("cv", [128, 4, 256])
    din("maskb", [128, 48])
    din("poolD", [4, 128, 30, 128], BF16)
    din("pool_w", [1, 4, 256, 256])
    din("pscl", [128, 8])
    din("cd_w_in", [1, D, 4096])
    din("cd_w_out", [1, D, D])
    din("hy_w1", [1, 33, 64])
    din("hy_w2", [1, 64, 64])
    din("hy_w3", [1, 64, 2048])
    din("hyfb", [64, 4])
    din("hyc", [128, 24, 4])
    din("hsk", [128, 2, 8])
    din("decb", [128, 2048])
    din("z0T", [33, T])
    din("negt", [128, 10])
    din("ffA", [TA, 2048], BF16)
    din("gtA", [TA, 2048], BF16)
    din("fiA", [2048, TA], BF16)
    din("ffB", [TB, 512], BF16)
    din("gtB", [TB, 512], BF16)
    din("fiB", [512, TB], BF16)
    dout("kvo", [T, 512])
    dout("yT", [D, T])
    dout("modout", [128, 2 * 9 * NCH * 2])

    with contextlib.ExitStack() as es:
        X = es.enter_context(nc.sbuf_tensor("X", [128, NCH, T], F32))
        H = es.enter_context(nc.sbuf_tensor("H", [128, NCH, T], BF16))
        ACTB = es.enter_context(nc.sbuf_tensor("ACTB", [128, GRP, T], BF16))
        WBUF = es.enter_context(nc.sbuf_tensor("WBUF", [128, NST, STAGE], BF16))
        SQ = es.enter_context(nc.sbuf_tensor("SQ", [128, 1, T], BF16))
        RSTD = es.enter_context(nc.sbuf_tensor("RSTD", [128, T], F32))
        TMP = es.enter_context(nc.sbuf_tensor("TMP", [128, 2, T], F32))
        MOD = es.enter_context(nc.sbuf_tensor("MOD", [128, 2, 9, NCH, 2], F32))
        MA = es.enter_context(nc.sbuf_tensor("MA", [128, 2, 3, NCH, 2], F32))
        MG = es.enter_context(nc.sbuf_tensor("MG", [128, 2, 3, NCH, 2], F32))
        NG = es.enter_context(nc.sbuf_tensor("NG", [128, 6, NCH], F32))
        BM = es.enter_context(nc.sbuf_tensor("BM", [128, 2, 9 * NCH], F32))
        COND = es.enter_context(nc.sbuf_tensor("COND", [128, 2, NCH], F32))
        ST = es.enter_context(nc.sbuf_tensor("ST", [128, NCH, 2], BF16))
        ONES = es.enter_context(nc.sbuf_tensor("ONES", [128, 128], BF16))
        IDF = es.enter_context(nc.sbuf_tensor("IDF", [128, 128], F32))
        PS = es.enter_context(nc.psum_tensor("PS", [128, 8 * 512], F32))
        KNB = es.enter_context(nc.sbuf_tensor("KNB", [128, 128], F32))
        Z2T = es.enter_context(nc.sbuf_tensor("Z2T", [64, T], BF16))
        MROW = Z2T[0:2, 0:1024].bitcast(F32).rearrange("p (o c) -> p o c", o=1)
        W3S = es.enter_context(nc.sbuf_tensor("W3S", [64, 256], BF16))
        HW1 = es.enter_context(nc.sbuf_tensor("HW1", [33, 64], F32))
        HW2 = es.enter_context(nc.sbuf_tensor("HW2", [64, 64], F32))
        HFB = es.enter_context(nc.sbuf_tensor("HFB", [64, 8], F32))
        NEGT = es.enter_context(nc.sbuf_tensor("NEGT", [128, 10], F32))
        HYC = es.enter_context(nc.sbuf_tensor("HYC", [128, 24, 4], F32))
        HSK = es.enter_context(nc.sbuf_tensor("HSK", [128, 2, 8], F32))
        IDB = es.enter_context(nc.sbuf_tensor("IDB", [128, 128], BF16))
        PSCL = es.enter_context(nc.sbuf_tensor("PSCL", [128, 8], F32))
        QKN = es.enter_context(nc.sbuf_tensor("QKN", [128, 2], F32))
        ROTM = es.enter_context(nc.sbuf_tensor("ROTM", [128, 128], F32))
        MASKB = es.enter_context(nc.sbuf_tensor("MASKB", [128, 48], F32))
        LRUC = es.enter_context(nc.sbuf_tensor("LRUC", [128, 8, 10], F32))
        LRUL = es.enter_context(nc.sbuf_tensor("LRUL", [128, 2, 8], F32))
        LRUL2 = es.enter_context(nc.sbuf_tensor("LRUL2", [128, 2, 8], F32))
        LRUH = es.enter_context(nc.sbuf_tensor("LRUH", [128, 2, 8], F32))
        CONT = es.enter_context(nc.sbuf_tensor("CONT", [128, 1], F32))
        INIT = es.enter_context(nc.sbuf_tensor("INIT", [128, 8], F32))
        STO = es.enter_context(nc.sbuf_tensor("STO", [128, 8, 5, 2], F32))
        XP = es.enter_context(nc.sbuf_tensor("XP", [128, 5, 259], F32))
        def actb_f32(i):
            return ACTB[:, 2 * i:2 * i + 2, :].rearrange("p a t -> p (a t)").bitcast(F32).rearrange("p (s t) -> p s t", s=5)
        XC, RB, IB = actb_f32(0), actb_f32(1), actb_f32(2)
        XCB = ACTB[:, 6, :]
        KVO = ACTB[:, 7, 0:1024].bitcast(F32).rearrange("p (o c) -> p o c", o=1)
        SS = es.enter_context(nc.sbuf_tensor("SS", [128, 4], F32))
        JUNK = TMP[:, 1, 0:128]

        def ps_tiles(slot):
            base = 3 * slot * 512
            return [PS[:, base:base + 512], PS[:, base + 512:base + 1024], PS[:, base + 1024:base + 1280]]

        def ps_AB(slot):
            base = 3 * slot * 512
            return PS[:, base:base + 1024], PS[:, base + 1024:base + 1280]

        PSX = PS[:, 6 * 512: 7 * 512]
        PSY = PS[:, 7 * 512: 8 * 512]

        plan = None
        for dry in (True, False):
            p = Prog(nc, es, dry=dry)
            ws = WStream(p, WBUF, plan)
            slotc = [0]

            def new_slot():
                s = slotc[0] % 2
                slotc[0] += 1
                return s

            def ld(dst, src, key, sem):
                def fn(eng, sm, dst=dst, src=src):
                    eng.dma_start(out=dst, in_=src).then_inc(sm, 16)
                p.dma('sp', fn, 1, sem, writes=(key,))

            xv = dr["xT"].rearrange("(c p) t -> p c t", p=128)
            for q in range(4):
                ld(X[:, 4 * q:4 * q + 4, :], xv[:, 4 * q:4 * q + 4, :], ('X', q), 'ldx%d' % q)
            for q in range(4):
                for c in range(4 * q, 4 * q + 4):
                    if not dry:
                        p.lastw[('Xc', c)] = p.lastw[('X', q)]
            cl_ = [(COND[:], dr["cond"], 'COND'), (NG[:], dr["normg"], 'NG'), (BM[:], dr["bmod"], 'BM'),
                   (KNB[:], dr["knb"], 'KNB'), (HSK[:], dr["hsk"], 'HSK'), (PSCL[:], dr["pscl"], 'PSCL'),
                   (QKN[:], dr["qkn"], 'QKN'), (ROTM[:], dr["rotm"], 'ROTM'), (MASKB[:], dr["maskb"], 'MASKB'),
                   (LRUC[:], dr["lruc"], 'LRUC'), (LRUL[:], dr["lrul"], 'LRUL'), (LRUH[:], dr["lruh"], 'LRUH'),
                   (CONT[:], dr["cont"], 'CONT')]
            def ldall(eng, sm):
                for dst_, src_, _ in cl_:
                    eng.dma_start(out=dst_, in_=src_).then_inc(sm, 16)
            p.dma('sp', ldall, len(cl_), 'ldc', writes=tuple(k_ for _, _, k_ in cl_))
            p.op('dve', lambda e: e.memset(XP[:], 0.0), writes=('XP',))
            p.op('dve', lambda e: e.memset(ONES[:], 1.0), writes=('ONES',))
            p.op('dve', lambda e: e.memset(IDF[:], 0.0), writes=('IDF',))
            def mk_ident(e):
                return e.affine_select(out=IDF[:], in_=IDF[:], pattern=[[-1, 128]], compare_op=ALU.not_equal,
                                       fill=1.0, base=0, channel_multiplier=1)
            p.op('pool', mk_ident, reads=('IDF',), writes=('IDF',))
            p.op('act', lambda e: e.copy(out=IDB[:], in_=IDF[:]), reads=('IDF',), writes=('IDB',))

            p.op('act', lambda e: e.activation(out=ST[:].rearrange("p c k -> p k c"), in_=COND[:], func=AF.Silu),
                 reads=('COND',), writes=('ST',))

            def modulation(l, qlo, qhi):
                for cb in range(qlo * 4, qhi * 4):
                    mod_block(l, cb)

            def mod_tasks(l, qlo, qhi):
                return [(lambda l=l, cb=cb: mod_block(l, cb)) for cb in range(qlo * 4, qhi * 4)]

            def mod_block(l, cb):
                for _ in range(1):
                    stg = []
                    for half in range(2):
                        src = dr["w_mod"][l, half * 1024:(half + 1) * 1024, cb * 512:(cb + 1) * 512] \
                            .rearrange("(k p) c -> p k c", p=128)
                        stg.append(ws.next(src, 8, 512))
                    if dry:
                        continue
                    pt = PSY
                    def mm(e, stg=stg, pt=pt):
                        ins = None
                        for kc in range(16):
                            v, _ = stg[kc // 8]
                            ins = e.matmul(pt[0:2, :], ST[:, kc, :], v[:, kc % 8, :], start=(kc == 0), stop=(kc == 15))
                        return ins
                    p.op('pe', mm, reads=('ST', stg[0][1], stg[1][1]), writes=('psy',))
                    par = 0
                    p.op('act', lambda e, pt=pt, par=par: e.copy(out=MROW[:, par, :], in_=pt[0:2, :]),
                         reads=('psy',), writes=(('MROW', par),))
                    def tr(e, par=par):
                        ins = None
                        for i in range(4):
                            ins = e.transpose(PSX[:, 2 * i:2 * i + 2], MROW[:, par, i * 128:(i + 1) * 128], IDF[0:2, 0:2])
                        return ins
                    p.op('pe', tr, reads=(('MROW', par), 'IDF'), writes=('psx',))
                    q, c0 = cb // 4, (cb % 4) * 4
                    def ev(e, q=q, c0=c0, l=l):
                        return e.tensor_tensor(
                            out=MOD[:, l, q, c0:c0 + 4, :],
                            in0=PSX[:, 0:8].rearrange("p (c k) -> p c k", k=2),
                            in1=BM[:, l, q * NCH + c0:q * NCH + c0 + 4].unsqueeze(2).to_broadcast([128, 4, 2]),
                            op=ALU.add)
                    p.op('dve', ev, reads=('psx', 'BM'), writes=(('MOD', l, q),))

            def mod_derive(l, j):
                def f1(e, l=l, j=j):
                    return e.scalar_tensor_tensor(
                        out=MA[:, l, j], in0=MOD[:, l, 3 * j + 1], scalar=1.0,
                        in1=NG[:, l * 3 + j, :].unsqueeze(2).to_broadcast([128, NCH, 2]),
                        op0=ALU.add, op1=ALU.mult)
                p.op('dve', f1, reads=(('MOD', l, 3 * j + 1), 'NG'), writes=(('MA', l, j),))
                sc = 1.0 if j == 1 else 0.5
                p.op('dve', lambda e, l=l, j=j, sc=sc: e.tensor_scalar(
                    out=MG[:, l, j], in0=MOD[:, l, 3 * j + 2], scalar1=sc, scalar2=None, op0=ALU.mult),
                    reads=(('MOD', l, 3 * j + 2),), writes=(('MG', l, j),))

            def norm_modulate(l, j):
                slot = new_slot()
                pts = ps_tiles(slot)
                SQ2 = RSTD[:, 0:640].bitcast(BF16)
                for c in range(NCH):
                    if c % 2 == 0:
                        buf, bkey = SQ[:, 0, :], ('SQ', 0)
                        p.op('act', lambda e, c=c, buf=buf: e.activation(out=buf, in_=X[:, c, :], func=AF.Square),
                             reads=(('Xc', c),), writes=(bkey,))
                    else:
                        buf, bkey = SQ2, 'RSTD'
                        p.op('dve', lambda e, c=c, buf=buf: e.tensor_tensor(out=buf, in0=X[:, c, :], in1=X[:, c, :], op=ALU.mult),
                             reads=(('Xc', c),), writes=(bkey,))
                    def mm(e, c=c, buf=buf):
                        ins = None
                        for ti, (t0, tw) in enumerate(tok_tiles()):
                            ins = e.matmul(pts[ti], ONES[:], buf[:, t0:t0 + tw], start=(c == 0), stop=(c == NCH - 1))
                        return ins
                    p.op('pe', mm, reads=(bkey, 'ONES'), writes=(('ps', slot),))
                pa, pb = ps_AB(slot)
                def ln(e):
                    e.activation(out=RSTD[:, 0:TA], in_=pa, func=AF.Ln, scale=1.0 / D, bias=1e-6)
                    return e.activation(out=RSTD[:, TA:T], in_=pb, func=AF.Ln, scale=1.0 / D, bias=1e-6)
                p.op('act', ln, reads=(('ps', slot),), writes=('RSTD',))
                p.op('act', lambda e: e.activation(out=RSTD[:], in_=RSTD[:], func=AF.Exp, scale=-0.5),
                     reads=('RSTD',), writes=('RSTD',))
                for c in range(NCH):
                    par = c % 2
                    def f1(e, c=c, par=par):
                        e.scalar_tensor_tensor(out=TMP[:, par, 0:TA], in0=X[:, c, 0:TA], scalar=MA[:, l, j, c, 0:1],
                                               in1=RSTD[:, 0:TA], op0=ALU.mult, op1=ALU.mult)
                        return e.scalar_tensor_tensor(out=TMP[:, par, TA:T], in0=X[:, c, TA:T], scalar=MA[:, l, j, c, 1:2],
                                                      in1=RSTD[:, TA:T], op0=ALU.mult, op1=ALU.mult)
                    p.op('dve', f1, reads=(('Xc', c), 'RSTD', ('MA', l, j)), writes=(('TMP', par),))
                    def f2(e, c=c, par=par):
                        e.activation(out=H[:, c, 0:TA], in_=TMP[:, par, 0:TA], func=AF.Identity,
                                     bias=MOD[:, l, 3 * j, c, 0:1])
                        return e.activation(out=H[:, c, TA:T], in_=TMP[:, par, TA:T], func=AF.Identity,
                                            bias=MOD[:, l, 3 * j, c, 1:2])
                    p.op('act', f2, reads=(('TMP', par), ('MOD', l, 3 * j)), writes=(('H', c),))

            def ffn(l, j, fi, bg=None):
                bg = list(bg or [])
                norm_modulate(l, j)
                w13 = dr["ffn_w13"][l, fi]
                w2 = dr["ffn_w2"][l, fi]
                Hkeys = tuple(('H', c) for c in range(NCH))
                hc0 = 0
                while hc0 < NHC:
                    ng = min(GRP, NHC - hc0)
                    for pr in range(ng // 2):
                        i0 = hc0 + 2 * pr
                        gsrc = w13[:, i0 * 128:(i0 + 2) * 128].rearrange("(k p) c -> p k c", p=128)
                        usrc = w13[:, DFF + i0 * 128:DFF + (i0 + 2) * 128].rearrange("(k p) c -> p k c", p=128)
                        gv, gk = ws.next(gsrc, 16, 256)
                        uv, uk = ws.next(usrc, 16, 256)
                        if dry:
                            if bg:
                                bg.pop(0)()
                            continue
                        for sub in range(2):
                            li = 2 * pr + sub
                            sg, su = new_slot(), new_slot()
                            for (sl, wv, wk) in ((sg, gv, gk), (su, uv, uk)):
                                pts = ps_tiles(sl)
                                def mm(e, wv=wv, pts=pts, sub=sub):
                                    ins = None
                                    for kc in range(NCH):
                                        for ti, (t0, tw) in enumerate(tok_tiles()):
                                            ins = e.matmul(pts[ti], wv[:, kc, sub * 128:(sub + 1) * 128], H[:, kc, t0:t0 + tw],
                                                           start=(kc == 0), stop=(kc == NCH - 1))
                                    return ins
                                p.op('pe', mm, reads=Hkeys + (wk,), writes=(('ps', sl),))
                            ga, gb = ps_AB(sg)
                            ua, ub = ps_AB(su)
                            par = li % 2
                            def silu(e, ga=ga, gb=gb, par=par):
                                e.activation(out=TMP[:, par, 0:TA], in_=ga, func=AF.Silu)
                                return e.activation(out=TMP[:, par, TA:T], in_=gb, func=AF.Silu)
                            p.op('act', silu, reads=(('ps', sg),), writes=(('TMP', par),))
                            def mul(e, ua=ua, ub=ub, par=par, li=li):
                                e.tensor_tensor(out=ACTB[:, li, 0:TA], in0=TMP[:, par, 0:TA], in1=ua, op=ALU.mult)
                                return e.tensor_tensor(out=ACTB[:, li, TA:T], in0=TMP[:, par, TA:T], in1=ub, op=ALU.mult)
                            p.op('dve', mul, reads=(('TMP', par), ('ps', su)), writes=(('ACTB', li),))
                        if bg:
                            bg.pop(0)()
                    Akeys = tuple(('ACTB', i) for i in range(ng))
                    for ob in range(4):
                        src = w2[hc0 * 128:(hc0 + ng) * 128, ob * 512:(ob + 1) * 512].rearrange("(k p) c -> p k c", p=128)
                        wv, wk = ws.next(src, ng, 512)
                        if dry:
                            if bg:
                                bg.pop(0)()
                            continue
                        for oc in range(4):
                            c = ob * 4 + oc
                            sl = new_slot()
                            pts = ps_tiles(sl)
                            def mm(e, wv=wv, pts=pts, oc=oc, ng=ng):
                                ins = None
                                for kc in range(ng):
                                    for ti, (t0, tw) in enumerate(tok_tiles()):
                                        ins = e.matmul(pts[ti], wv[:, kc, oc * 128:(oc + 1) * 128], ACTB[:, kc, t0:t0 + tw],
                                                       start=(kc == 0), stop=(kc == ng - 1))
                                return ins
                            p.op('pe', mm, reads=Akeys + (wk,), writes=(('ps', sl),))
                            oa, ob_ = ps_AB(sl)
                            def acc(e, oa=oa, ob_=ob_, c=c):
                                e.scalar_tensor_tensor(out=X[:, c, 0:TA], in0=oa, scalar=MG[:, l, j, c, 0:1],
                                                       in1=X[:, c, 0:TA], op0=ALU.mult, op1=ALU.add)
                                return e.scalar_tensor_tensor(out=X[:, c, TA:T], in0=ob_, scalar=MG[:, l, j, c, 1:2],
                                                              in1=X[:, c, TA:T], op0=ALU.mult, op1=ALU.add)
                            p.op('dve', acc, reads=(('ps', sl), ('Xc', c), ('MG', l, j)), writes=(('Xc', c),))
                        if bg:
                            bg.pop(0)()
                    hc0 += ng
                while bg:
                    bg.pop(0)()

            def kv_cache_out(e_idx):
                win = dr["ab_w_in"][e_idx]
                stg = []
                for half in range(2):
                    src = win[half * 1024:(half + 1) * 1024, 1024:1536].rearrange("(k p) c -> p k c", p=128)
                    stg.append(ws.next(src, 8, 512))
                if dry:
                    return
                Hkeys = tuple(('H', c) for c in range(NCH))
                for tt in range(T // 128):
                    par = 0
                    bank = PSX if tt % 2 == 0 else PSY
                    bkey = 'psx' if tt % 2 == 0 else 'psy'
                    def mm(e, tt=tt, bank=bank):
                        ins = None
                        for kc in range(NCH):
                            v, _ = stg[kc // 8]
                            ins = e.matmul(bank, H[:, kc, tt * 128:(tt + 1) * 128], v[:, kc % 8, :],
                                           start=(kc == 0), stop=(kc == NCH - 1))
                        return ins
                    p.op('pe', mm, reads=Hkeys + (stg[0][1], stg[1][1]), writes=(bkey,))
                    p.op('act', lambda e, bank=bank, par=par: e.copy(out=KVO[:, par, 256:512], in_=bank[:, 256:512]),
                         reads=(bkey,), writes=(('KVO', par, 'v'),))
                    def sq(e, bank=bank, par=par):
                        e.activation(out=JUNK, in_=bank[:, 0:128], func=AF.Square, accum_out=SS[:, 2 * par:2 * par + 1])
                        return e.activation(out=JUNK, in_=bank[:, 128:256], func=AF.Square, accum_out=SS[:, 2 * par + 1:2 * par + 2])
                    p.op('act', sq, reads=(bkey,), writes=('JUNK', ('SS', par)))
                    p.op('act', lambda e, par=par: e.activation(out=SS[:, 2 * par:2 * par + 2], in_=SS[:, 2 * par:2 * par + 2],
                                                               func=AF.Ln, scale=1.0 / 128, bias=1e-6),
                         reads=(('SS', par),), writes=(('SS', par),))
                    p.op('act', lambda e, par=par: e.activation(out=SS[:, 2 * par:2 * par + 2], in_=SS[:, 2 * par:2 * par + 2],
                                                               func=AF.Exp, scale=-0.5),
                         reads=(('SS', par),), writes=(('SS', par),))
                    def kn(e, bank=bank, par=par):
                        e.scalar_tensor_tensor(out=KVO[:, par, 0:128], in0=bank[:, 0:128], scalar=SS[:, 2 * par:2 * par + 1],
                                               in1=KNB[:], op0=ALU.mult, op1=ALU.mult)
                        return e.scalar_tensor_tensor(out=KVO[:, par, 128:256], in0=bank[:, 128:256],
                                                      scalar=SS[:, 2 * par + 1:2 * par + 2], in1=KNB[:], op0=ALU.mult, op1=ALU.mult)
                    p.op('dve', kn, reads=(bkey, ('SS', par), 'KNB'), writes=(('KVO', par, 'k'),))
                    def st(eng, sm, tt=tt, par=par):
                        eng.dma_start(out=dr["kvo"][tt * 128:(tt + 1) * 128, :], in_=KVO[:, par, :]).then_inc(sm, 16)
                    p.dma('sp', st, 1, 'stkv%d' % par, reads=(('KVO', par, 'k'), ('KVO', par, 'v')), is_output=True)

            def wout_incr(rows_ap, act_ap, act_key, l):
                acts = act_ap if isinstance(act_ap, (list, tuple)) else [act_ap]
                keys = tuple(act_key) if isinstance(act_key, list) else (act_key,)
                nk = rows_ap.shape[0] // 128
                wv, wk = ws.next(rows_ap.rearrange("(k p) c -> p k c", p=128), nk, D)
                if dry:
                    return
                for oc in range(NCH):
                    sl = new_slot()
                    pts = ps_tiles(sl)
                    def mm(e, wv=wv, pts=pts, oc=oc):
                        ins = None
                        for k in range(nk):
                            for ti, (t0, tw) in enumerate(tok_tiles()):
                                ins = e.matmul(pts[ti], wv[:, k, oc * 128:(oc + 1) * 128], acts[k][:, t0:t0 + tw],
                                               start=(k == 0), stop=(k == nk - 1))
                        return ins
                    p.op('pe', mm, reads=keys + (wk,), writes=(('ps', sl),))
                    oa, ob_ = ps_AB(sl)
                    def acc(e, oa=oa, ob_=ob_, oc=oc):
                        e.scalar_tensor_tensor(out=X[:, oc, 0:TA], in0=oa, scalar=MG[:, l, 1, oc, 0:1],
                                               in1=X[:, oc, 0:TA], op0=ALU.mult, op1=ALU.add)
                        return e.scalar_tensor_tensor(out=X[:, oc, TA:T], in0=ob_, scalar=MG[:, l, 1, oc, 1:2],
                                                      in1=X[:, oc, TA:T], op0=ALU.mult, op1=ALU.add)
                    p.op('dve', acc, reads=(('ps', sl), ('Xc', oc), ('MG', l, 1)), writes=(('Xc', oc),))

            def lru(e_idx):
                win = dr["ab_w_in"][e_idx]
                Hkeys = tuple(('H', c) for c in range(NCH))
                p.op('act', lambda e: e.activation(out=LRUL[:], in_=LRUL[:], func=AF.Exp, scale=-1.0),
                     reads=('LRUL',), writes=('LRUL',))
                p.op('act', lambda e: e.activation(out=LRUL[:], in_=LRUL[:], func=AF.Ln, bias=1.0),
                     reads=('LRUL',), writes=('LRUL',))
                p.op('dve', lambda e: e.tensor_scalar(out=LRUL2[:], in0=LRUL[:], scalar1=-16.0, scalar2=None, op0=ALU.mult),
                     reads=('LRUL',), writes=('LRUL2',))
                p.op('dve', lambda e: e.tensor_scalar(out=LRUL[:], in0=LRUL[:], scalar1=-8.0, scalar2=None, op0=ALU.mult),
                     reads=('LRUL', 'LRUL2'), writes=('LRUL',))
                AB = RSTD[:].rearrange("p (s t) -> p s t", s=5)
                HF = TMP[:, 0, :].rearrange("p (s t) -> p s t", s=5)
                HB = TMP[:, 1, :].rearrange("p (s t) -> p s t", s=5)
                for c in range(8):
                    src = win[:, 1536 + c * 128:1536 + (c + 1) * 128].rearrange("(k p) c -> p k c", p=128)
                    wv, wk = ws.next(src, 16, 128)
                    gw, gwk = ws.next(dr["lru_gate_w"][c], 4, 128)
                    if dry:
                        ws.next(win[:, 2560 + c * 128:2560 + (c + 1) * 128].rearrange("(k p) c -> p k c", p=128), 16, 128)
                        if c % 2 == 1:
                            wout_incr(dr["ab_w_out"][e_idx][1024 + (c - 1) * 128:1024 + (c + 1) * 128, :], [None, None], [None, None], 0)
                        continue
                    for sub in range(1):
                        sl = new_slot()
                        pts = ps_tiles(sl)
                        def mm(e, wv=wv, pts=pts, sub=sub):
                            ins = None
                            for kc in range(NCH):
                                for ti, (t0, tw) in enumerate(tok_tiles()):
                                    ins = e.matmul(pts[ti], wv[:, kc, :], H[:, kc, t0:t0 + tw],
                                                   start=(kc == 0), stop=(kc == NCH - 1))
                            return ins
                        p.op('pe', mm, reads=Hkeys + (wk,), writes=(('ps', sl),))
                        pa, pb = ps_AB(sl)
                        def cp_in(e, pa=pa, pb=pb):
                            e.copy(out=XP[:, 0:4, 2:258], in_=pa.rearrange("p (s t) -> p s t", s=4))
                            return e.copy(out=XP[:, 4, 2:258], in_=pb)
                        p.op('act', cp_in, reads=(('ps', sl),), writes=('XP',))
                        def halo(e):
                            e.tensor_scalar(out=XP[:, 1:4, 0:2], in0=XP[:, 0:3, 256:258], scalar1=CONT[:, 0:1], scalar2=None, op0=ALU.mult)
                            return e.tensor_scalar(out=XP[:, 0:3, 258:259], in0=XP[:, 1:4, 2:3], scalar1=CONT[:, 0:1], scalar2=None, op0=ALU.mult)
                        p.op('dve', halo, reads=('XP', 'CONT'), writes=('XP',))
                        def conv(e, c=c):
                            e.tensor_scalar(out=XC, in0=XP[:, :, 0:256], scalar1=LRUC[:, c, 0:1], scalar2=LRUC[:, c, 4:5],
                                            op0=ALU.mult, op1=ALU.add)
                            ins = None
                            for jt in range(1, 4):
                                ins = e.scalar_tensor_tensor(out=XC, in0=XP[:, :, jt:jt + 256], scalar=LRUC[:, c, jt:jt + 1],
                                                             in1=XC, op0=ALU.mult, op1=ALU.add)
                            return ins
                        p.op('dve', conv, reads=('XP', 'LRUC'), writes=('XC',))
                        p.op('act', lambda e: e.copy(out=XCB, in_=XC.rearrange("p s t -> p (s t)")),
                             reads=('XC',), writes=('XCB',))
                        for d in range(2):
                            outs = (RB, IB)
                            for g in range(2):
                                sl = new_slot()
                                pts = ps_tiles(sl)
                                gi = d * 2 + g
                                def gm(e, pts=pts, gi=gi, gw=gw, sub=sub):
                                    ins = None
                                    for ti, (t0, tw) in enumerate(tok_tiles()):
                                        ins = e.matmul(pts[ti], gw[:, gi, :], XCB[:, t0:t0 + tw], start=True, stop=True)
                                    return ins
                                p.op('pe', gm, reads=('XCB', gwk), writes=(('ps', sl),))
                                pa, pb = ps_AB(sl)
                                dst = outs[g]
                                def sg(e, pa=pa, pb=pb, dst=dst, c=c, d=d, g=g):
                                    bias = LRUC[:, c, 5 + d * 2 + g:6 + d * 2 + g]
                                    e.activation(out=dst[:, 0:4, :], in_=pa.rearrange("p (s t) -> p s t", s=4), func=AF.Sigmoid, bias=bias)
                                    return e.activation(out=dst[:, 4, :], in_=pb, func=AF.Sigmoid, bias=bias)
                                p.op('act', sg, reads=(('ps', sl), 'LRUC'), writes=('RB' if g == 0 else 'IB',))
                            p.op('act', lambda e, c=c, d=d: e.activation(out=AB, in_=RB, func=AF.Exp, scale=LRUL[:, d, c:c + 1]),
                                 reads=('RB', 'LRUL'), writes=('AB',))
                            p.op('act', lambda e, c=c, d=d: e.activation(out=RB, in_=RB, func=AF.Exp, scale=LRUL2[:, d, c:c + 1]),
                                 reads=('RB', 'LRUL2'), writes=('RB',))
                            p.op('act', lambda e: e.activation(out=RB, in_=RB, func=AF.Sqrt, scale=-1.0, bias=1.0),
                                 reads=('RB',), writes=('RB',))
                            def bmul(e):
                                e.tensor_tensor(out=IB, in0=IB, in1=XC, op=ALU.mult)
                                return e.tensor_tensor(out=IB, in0=IB, in1=RB, op=ALU.mult)
                            p.op('dve', bmul, reads=('IB', 'XC', 'RB'), writes=('IB',))
                            HD = HF if d == 0 else HB
                            hk = 'HF' if d == 0 else 'HB'
                            order = range(5) if d == 0 else range(4, -1, -1)
                            for sgi in order:
                                init = None
                                if d == 0:
                                    if sgi == 0:
                                        init = LRUH[:, 0, c:c + 1]
                                    elif sgi == 4:
                                        init = 0.0
                                    else:
                                        src_col = HD[:, sgi - 1, 255:256]
                                else:
                                    if sgi == 4:
                                        init = 0.0
                                    elif sgi == 3:
                                        init = LRUH[:, 1, c:c + 1]
                                    else:
                                        src_col = HD[:, sgi + 1, 0:1]
                                if init is None:
                                    p.op('dve', lambda e, sgi=sgi, src_col=src_col: e.tensor_scalar(
                                        out=INIT[:, sgi:sgi + 1], in0=src_col, scalar1=CONT[:, 0:1], scalar2=None, op0=ALU.mult),
                                        reads=(hk, 'CONT'), writes=('INIT',))
                                    init = INIT[:, sgi:sgi + 1]
                                if d == 0:
                                    fn = lambda e, sgi=sgi, init=init, HD=HD: e.tensor_tensor_scan(
                                        out=HD[:, sgi, :], data0=AB[:, sgi, :], data1=IB[:, sgi, :], initial=init, op0=ALU.mult, op1=ALU.add)
                                else:
                                    fn = lambda e, sgi=sgi, init=init, HD=HD: e.tensor_tensor_scan(
                                        out=HD[:, sgi, ::-1], data0=AB[:, sgi, ::-1], data1=IB[:, sgi, ::-1], initial=init,
                                        op0=ALU.mult, op1=ALU.add)
                                p.op('dve', fn, reads=('AB', 'IB', 'LRUH', 'INIT'), writes=(hk,))
                            col = 255 if d == 0 else 0
                            p.op('act', lambda e, c=c, d=d, HD=HD, col=col: e.copy(out=STO[:, c, :, d], in_=HD[:, :, col]),
                                 reads=(hk,), writes=('STO',))
                        lsrc = win[:, 2560 + c * 128:2560 + (c + 1) * 128].rearrange("(k p) c -> p k c", p=128)
                        lv, lk = ws.next(lsrc, 16, 128)
                        sl = new_slot()
                        pts = ps_tiles(sl)
                        def mmg(e, lv=lv, pts=pts):
                            ins = None
                            for kc in range(NCH):
                                for ti, (t0, tw) in enumerate(tok_tiles()):
                                    ins = e.matmul(pts[ti], lv[:, kc, :], H[:, kc, t0:t0 + tw], start=(kc == 0), stop=(kc == NCH - 1))
                            return ins
                        p.op('pe', mmg, reads=Hkeys + (lk,), writes=(('ps', sl),))
                        pa, pb = ps_AB(sl)
                        G1 = RB.rearrange("p s t -> p (s t)")
                        def g_sq(e, pa=pa, pb=pb):
                            e.activation(out=G1[:, 0:TA], in_=pa, func=AF.Square)
                            return e.activation(out=G1[:, TA:T], in_=pb, func=AF.Square)
                        p.op('act', g_sq, reads=(('ps', sl),), writes=('RB',))
                        def g_poly(e, pa=pa, pb=pb):
                            e.tensor_scalar(out=G1, in0=G1, scalar1=0.044715, scalar2=1.0, op0=ALU.mult, op1=ALU.add)
                            e.tensor_tensor(out=G1[:, 0:TA], in0=G1[:, 0:TA], in1=pa, op=ALU.mult)
                            return e.tensor_tensor(out=G1[:, TA:T], in0=G1[:, TA:T], in1=pb, op=ALU.mult)
                        p.op('dve', g_poly, reads=('RB', ('ps', sl)), writes=('RB',))
                        p.op('act', lambda e: e.activation(out=G1, in_=G1, func=AF.Sigmoid, scale=1.5957691216057308),
                             reads=('RB',), writes=('RB',))
                        RECB = ACTB[:, 7, :] if c % 2 == 0 else SQ[:, 0, :]
                        rkey = 'REC' if c % 2 == 0 else ('SQ', 0)
                        def g_fin(e, pa=pa, pb=pb, RECB=RECB):
                            e.tensor_tensor(out=G1[:, 0:TA], in0=G1[:, 0:TA], in1=pa, op=ALU.mult)
                            e.tensor_tensor(out=G1[:, TA:T], in0=G1[:, TA:T], in1=pb, op=ALU.mult)
                            e.tensor_tensor(out=TMP[:, 0, :], in0=TMP[:, 0, :], in1=TMP[:, 1, :], op=ALU.add)
                            return e.tensor_tensor(out=RECB, in0=G1, in1=TMP[:, 0, :], op=ALU.mult)
                        p.op('dve', g_fin, reads=('RB', ('ps', sl), 'HF', 'HB'), writes=('RB', 'HF', rkey))
                        if c % 2 == 1:
                            wout_incr(dr["ab_w_out"][e_idx][1024 + (c - 1) * 128:1024 + (c + 1) * 128, :],
                                      [ACTB[:, 7, :], SQ[:, 0, :]], ['REC', ('SQ', 0)], 0)
                def sst(eng, sm):
                    eng.dma_start(out=dr["sto"], in_=STO[:].rearrange("p a b c -> p (a b c)")).then_inc(sm, 16)
                p.dma('sp', sst, 1, 'stst', reads=('STO',), is_output=True)

            def attention(e_idx):
                win = dr["ab_w_in"][e_idx]
                Hkeys = tuple(('H', c) for c in range(NCH))
                KT = ACTB[:, 0:2, :]
                KTC = ACTB[:, 2, 0:1024].rearrange("p (g s) -> p g s", g=2)
                VT = ACTB[:, 3:6, :].rearrange("p a t -> p (a t)")[:, 0:3584].rearrange("p (c d) -> p c d", c=14)
                COS, SIN = ACTB[:, 6, 0:TA], ACTB[:, 7, 0:TA]
                QTH = SQ[:, 0, :]
                QN = TMP[:, 0, :]
                ATT = TMP[:, 0, 0:640].bitcast(BF16)
                RDEN = TMP[:, 1, 0:TA]
                xpf = XP[:].rearrange("p s t -> p (s t)")[:, 0:1024].bitcast(BF16)
                PT = [xpf[:, 0:1024], xpf[:, 1024:2048]]
                SC = 128 ** -0.5
                if not dry:
                    def ldt(eng, sm):
                        eng.dma_start(out=ACTB[:, 6:8, 0:TA], in_=dr["cossin"]).then_inc(sm, 16)
                        eng.dma_start(out=KTC, in_=dr["ckT"]).then_inc(sm, 16)
                        eng.dma_start(out=VT[:, 0:4, :], in_=dr["cv"]).then_inc(sm, 16)
                    p.dma('pool', ldt, 3, 'ldatt', writes=('CS', 'KTC', 'VT'))
                vsrc = win[:, 1280:1536].rearrange("(k p) c -> p k c", p=128)
                vv, vk = ws.next(vsrc, 16, 256)
                if not dry:
                    for tt in range(T // 128):
                        bank = PSX if tt % 2 == 0 else PSY
                        bkey = 'psx' if tt % 2 == 0 else 'psy'
                        def mm(e, tt=tt, bank=bank):
                            ins = None
                            for kc in range(NCH):
                                ins = e.matmul(bank[:, 0:256], H[:, kc, tt * 128:(tt + 1) * 128], vv[:, kc, :],
                                               start=(kc == 0), stop=(kc == NCH - 1))
                            return ins
                        p.op('pe', mm, reads=Hkeys + (vk,), writes=(bkey,))
                        p.op('act', lambda e, bank=bank, tt=tt: e.copy(out=VT[:, 4 + tt, :], in_=bank[:, 0:256]),
                             reads=(bkey,), writes=('VT',))

                def qk_head(wv, wk, coff, gcol, dst, dkey):
                    s1 = new_slot()
                    pts = ps_tiles(s1)
                    def mm(e):
                        ins = None
                        for kc in range(NCH):
                            for ti, (t0, tw) in enumerate(tok_tiles()):
                                ins = e.matmul(pts[ti], wv[:, kc, coff:coff + 128], H[:, kc, t0:t0 + tw],
                                               start=(kc == 0), stop=(kc == NCH - 1))
                        return ins
                    p.op('pe', mm, reads=Hkeys + (wk,), writes=(('ps', s1),))
                    pa, pb = ps_AB(s1)
                    def sq(e):
                        e.activation(out=SQ[:, 0, 0:TA], in_=pa, func=AF.Square)
                        return e.activation(out=SQ[:, 0, TA:T], in_=pb, func=AF.Square)
                    p.op('act', sq, reads=(('ps', s1),), writes=(('SQ', 0),))
                    s2 = new_slot()
                    pts2 = ps_tiles(s2)
                    def mm2(e):
                        ins = None
                        for ti, (t0, tw) in enumerate(tok_tiles()):
                            ins = e.matmul(pts2[ti], ONES[:], SQ[:, 0, t0:t0 + tw], start=True, stop=True)
                        return ins
                    p.op('pe', mm2, reads=(('SQ', 0), 'ONES'), writes=(('ps', s2),))
                    pa2, pb2 = ps_AB(s2)
                    def ln(e):
                        e.activation(out=RSTD[:, 0:TA], in_=pa2, func=AF.Ln, scale=1.0 / 128, bias=1e-6)
                        e.activation(out=RSTD[:, TA:T], in_=pb2, func=AF.Ln, scale=1.0 / 128, bias=1e-6)
                        return e.activation(out=RSTD[:], in_=RSTD[:], func=AF.Exp, scale=-0.5)
                    p.op('act', ln, reads=(('ps', s2),), writes=('RSTD',))
                    def qn(e):
                        e.scalar_tensor_tensor(out=QN[:, 0:TA], in0=pa, scalar=QKN[:, gcol:gcol + 1], in1=RSTD[:, 0:TA],
                                               op0=ALU.mult, op1=ALU.mult)
                        return e.scalar_tensor_tensor(out=QN[:, TA:T], in0=pb, scalar=QKN[:, gcol:gcol + 1], in1=RSTD[:, TA:T],
                                                      op0=ALU.mult, op1=ALU.mult)
                    p.op('dve', qn, reads=(('ps', s1), 'RSTD', 'QKN'), writes=('QN',))
                    s3 = new_slot()
                    pts3 = ps_tiles(s3)
                    def mm3(e):
                        e.matmul(pts3[0], ROTM[:], QN[:, 0:512], start=True, stop=True)
                        return e.matmul(pts3[1], ROTM[:], QN[:, 512:1024], start=True, stop=True)
                    p.op('pe', mm3, reads=('QN', 'ROTM'), writes=(('ps', s3),))
                    pa3, _ = ps_AB(s3)
                    def rope(e):
                        e.tensor_tensor(out=TMP[:, 1, 0:TA], in0=QN[:, 0:TA], in1=COS, op=ALU.mult)
                        e.tensor_tensor(out=RSTD[:, 0:TA], in0=pa3, in1=SIN, op=ALU.mult)
                        return e.tensor_tensor(out=dst[:, 0:TA], in0=TMP[:, 1, 0:TA], in1=RSTD[:, 0:TA], op=ALU.add)
                    p.op('dve', rope, reads=('QN', 'CS', ('ps', s3)), writes=('T1', 'RSTD', dkey))
                    p.op('act', lambda e: e.copy(out=dst[:, TA:T], in_=QN[:, TA:T]), reads=('QN',), writes=(dkey,))

                ksrc = win[:, 1024:1280].rearrange("(k p) c -> p k c", p=128)
                kv_, kk = ws.next(ksrc, 16, 256)
                if not dry:
                    for g in range(2):
                        qk_head(kv_, kk, g * 128, 1, KT[:, g, :], 'KT')

                def attn_head(h):
                    g = h // 4
                    for sc in range(12):
                        par = sc % 2
                        skey = 'pS0' if par == 0 else 'pS1'
                        Sps = PS[:, 0:1024] if par == 0 else PS[:, 6 * 512:8 * 512]
                        kT = KTC[:, g, sc * 128:(sc + 1) * 128] if sc < 4 else KT[:, g, (sc - 4) * 128:(sc - 3) * 128]
                        def smm(e, Sps=Sps, kT=kT):
                            e.matmul(Sps[:, 0:512], kT, QTH[:, 0:512], start=True, stop=True)
                            return e.matmul(Sps[:, 512:1024], kT, QTH[:, 512:1024], start=True, stop=True)
                        p.op('pe', smm, reads=('QTH', 'KT', 'KTC'), writes=(skey,))
                        def ex(e, Sps=Sps, par=par, sc=sc):
                            ins = None
                            for qs in range(4):
                                ins = e.activation(out=PT[par][:, qs * 256:(qs + 1) * 256], in_=Sps[:, qs * 256:(qs + 1) * 256],
                                                   func=AF.Exp, scale=SC, bias=MASKB[:, sc * 4 + qs:sc * 4 + qs + 1])
                            return ins
                        p.op('act', ex, reads=(skey, 'MASKB'), writes=('PT%d' % par,))
                        def omm(e, par=par, sc=sc, g=g):
                            st, sp = (sc == 0), (sc == 11)
                            e.matmul(PS[:, 1024:1536], VT[:, sc, g * 128:(g + 1) * 128], PT[par][:, 0:512], start=st, stop=sp)
                            e.matmul(PS[:, 1536:2048], VT[:, sc, g * 128:(g + 1) * 128], PT[par][:, 512:1024], start=st, stop=sp)
                            e.matmul(PS[:, 2048:2560], ONES[:], PT[par][:, 0:512], start=st, stop=sp)
                            return e.matmul(PS[:, 2560:3072], ONES[:], PT[par][:, 512:1024], start=st, stop=sp)
                        p.op('pe', omm, reads=('PT%d' % par, 'VT', 'ONES'), writes=('pO', 'pD'))
                    def rd(e):
                        e.activation(out=RDEN, in_=PS[:, 2048:3072], func=AF.Ln)
                        return e.activation(out=RDEN, in_=RDEN, func=AF.Exp, scale=-1.0)
                    p.op('act', rd, reads=('pD',), writes=('RDEN',))
                    p.op('dve', lambda e: e.tensor_tensor(out=ATT[:, 0:TA], in0=PS[:, 1024:2048], in1=RDEN, op=ALU.mult),
                         reads=('pO', 'RDEN'), writes=('ATT',))
                    for j in range(2):
                        par = j % 2
                        skey = 'pS0' if par == 0 else 'pS1'
                        Sps = PS[:, 0:256] if par == 0 else PS[:, 6 * 512:6 * 512 + 256]
                        kT = KT[:, g, TA + j * 128:TA + (j + 1) * 128]
                        p.op('pe', lambda e, Sps=Sps, kT=kT: e.matmul(Sps, kT, QTH[:, TA:T], start=True, stop=True),
                             reads=('QTH', 'KT'), writes=(skey,))
                        p.op('act', lambda e, Sps=Sps, par=par: e.activation(out=PT[par][:, 0:256], in_=Sps, func=AF.Exp, scale=SC),
                             reads=(skey,), writes=('PT%d' % par,))
                        def omb(e, par=par, j=j, g=g):
                            e.matmul(PS[:, 1024:1280], VT[:, 12 + j, g * 128:(g + 1) * 128], PT[par][:, 0:256], start=(j == 0), stop=(j == 1))
                            return e.matmul(PS[:, 2048:2304], ONES[:], PT[par][:, 0:256], start=(j == 0), stop=(j == 1))
                        p.op('pe', omb, reads=('PT%d' % par, 'VT', 'ONES'), writes=('pO', 'pD'))
                    def rdb(e):
                        e.activation(out=RDEN[:, 0:TB], in_=PS[:, 2048:2304], func=AF.Ln)
                        return e.activation(out=RDEN[:, 0:TB], in_=RDEN[:, 0:TB], func=AF.Exp, scale=-1.0)
                    p.op('act', rdb, reads=('pD',), writes=('RDEN',))
                    p.op('dve', lambda e: e.tensor_tensor(out=ATT[:, TA:T], in0=PS[:, 1024:1280], in1=RDEN[:, 0:TB], op=ALU.mult),
                         reads=('pO', 'RDEN'), writes=('ATT',))

                for hp in range(4):
                    qsrc = win[:, hp * 256:(hp + 1) * 256].rearrange("(k p) c -> p k c", p=128)
                    qv, qk = ws.next(qsrc, 16, 256)
                    for sub in range(2):
                        h = 2 * hp + sub
                        if not dry:
                            qk_head(qv, qk, sub * 128, 0, QTH, 'QTH')
                            attn_head(h)
                        wout_incr(dr["ab_w_out"][e_idx][h * 128:(h + 1) * 128, :], ATT, 'ATT', 0)

            def mixer_even(l):
                norm_modulate(l, 1)
                kv_cache_out(l // 2)
                lru(l // 2)
                import os
                if not os.environ.get('NOATT'):
                    attention(l // 2)

            def mixer_odd(l):
                o_idx = l // 2
                norm_modulate(l, 1)
                win = dr["cd_w_in"][o_idx]
                Hkeys = tuple(('H', c) for c in range(NCH))
                Wsp = ACTB[:, 0:4, :].rearrange("p a t -> p (a t)").rearrange("p (c d) -> p c d", c=20)
                VB = ACTB[:, 4:6, :]
                X1B = ACTB[:, 6:8, :]
                CACC = TMP[:, 0, :].rearrange("p (s t) -> p s t", s=5)
                UT = TMP[:, 0, :].bitcast(BF16).rearrange("p (c d) -> p c d", c=10)
                FT = TMP[:, 1, :].bitcast(BF16).rearrange("p (c d) -> p c d", c=10)
                TT = TMP[:, 1, :]
                KFS, RSA, RSB, DECS, ES = (RSTD[:, i * 256:(i + 1) * 256] for i in range(5))
                ZF = SQ[:, 0, :]
                XP3 = XP[:, :, 0:258]
                ABSF = XP[:].rearrange("p s t -> p (s t)")[:, 0:128].bitcast(BF16)
                PSXB, PSYB = PSX.bitcast(BF16), PSY.bitcast(BF16)
                Wk = tuple(('ACTB', i) for i in range(4))
                VBk = (('ACTB', 4), ('ACTB', 5))
                X1k = (('ACTB', 6), ('ACTB', 7))

                if not dry:
                    p.op('dve', lambda e: e.memset(XP[:], 0.0), writes=('XP',))
                    Z0 = TMP[0:33, 1, :]
                    Z1T = TMP[0:64, 0, :]
                    def ldz(eng, sm):
                        eng.dma_start(out=Z0, in_=dr["z0T"]).then_inc(sm, 16)
                        eng.dma_start(out=HW1[:], in_=dr["hy_w1"][o_idx]).then_inc(sm, 16)
                        eng.dma_start(out=HW2[:], in_=dr["hy_w2"][o_idx]).then_inc(sm, 16)
                        eng.dma_start(out=HFB[:, 0:4], in_=dr["hyfb"]).then_inc(sm, 16)
                        eng.dma_start(out=NEGT[:], in_=dr["negt"]).then_inc(sm, 16)
                        eng.dma_start(out=HYC[:], in_=dr["hyc"]).then_inc(sm, 16)
                    p.dma('sp', ldz, 6, 'ldhy', writes=(('TMP', 1), 'HW', 'HFB', 'NEGT', 'HYC'))
                    p.op('dve', lambda e: e.tensor_scalar(out=HFB[:, 4:6], in0=HFB[:, 2:4], scalar1=1.0 / 3, scalar2=None, op0=ALU.mult),
                         reads=('HFB',), writes=('HFB',))
                    p.op('dve', lambda e: e.tensor_tensor(out=HFB[:, 6:8], in0=HFB[:, 4:6], in1=HFB[:, 0:2], op=ALU.mult),
                         reads=('HFB',), writes=('HFB',))
                    for layer in range(2):
                        src = Z0 if layer == 0 else Z1T
                        wl = HW1[0:33, :] if layer == 0 else HW2[0:64, :]
                        skey = ('TMP', 1) if layer == 0 else ('TMP', 0)
                        sl = new_slot()
                        pts = ps_tiles(sl)
                        def mmz(e, src=src, wl=wl, pts=pts):
                            ins = None
                            for ti, (t0, tw) in enumerate(tok_tiles()):
                                ins = e.matmul(pts[ti][0:64, :], wl, src[:, t0:t0 + tw], start=True, stop=True)
                            return ins
                        p.op('pe', mmz, reads=(skey, 'HW'), writes=(('ps', sl),))
                        S1 = RSTD[0:64, :]
                        pa, pb = ps_AB(sl)
                        def sn(e, pa=pa, pb=pb, layer=layer):
                            e.activation(out=S1[:, 0:TA], in_=pa[0:64, :], func=AF.Sin, scale=HFB[:, 4 + layer:5 + layer],
                                         bias=HFB[:, 6 + layer:7 + layer])
                            return e.activation(out=S1[:, TA:T], in_=pb[0:64, :], func=AF.Sin, scale=HFB[:, 4 + layer:5 + layer],
                                                bias=HFB[:, 6 + layer:7 + layer])
                        p.op('act', sn, reads=(('ps', sl), 'HFB'), writes=('RSTD',))
                        S2 = TMP[0:64, 1, :]
                        p.op('dve', lambda e: e.tensor_tensor(out=S2, in0=S1, in1=S1, op=ALU.mult), reads=('RSTD',), writes=(('TMP', 1),))
                        p.op('dve', lambda e: e.tensor_scalar(out=S2, in0=S2, scalar1=-4.0, scalar2=3.0, op0=ALU.mult, op1=ALU.add),
                             reads=(('TMP', 1),), writes=(('TMP', 1),))
                        dstz = Z1T if layer == 0 else Z2T[:]
                        dk = ('TMP', 0) if layer == 0 else 'Z2T'
                        p.op('dve', lambda e, dstz=dstz: e.tensor_tensor(out=dstz, in0=S2, in1=S1, op=ALU.mult),
                             reads=(('TMP', 1), 'RSTD'), writes=(dk,))

                def conv_chunk(col0, ci, dst, dkeys):
                    src = win[:, col0:col0 + 128].rearrange("(k p) c -> p k c", p=128)
                    wv, wk = ws.next(src, 16, 128)
                    if dry:
                        return
                    sl = new_slot()
                    pts = ps_tiles(sl)
                    def mm(e):
                        ins = None
                        for kc in range(NCH):
                            for ti, (t0, tw) in enumerate(tok_tiles()):
                                ins = e.matmul(pts[ti], wv[:, kc, :], H[:, kc, t0:t0 + tw], start=(kc == 0), stop=(kc == NCH - 1))
                        return ins
                    p.op('pe', mm, reads=Hkeys + (wk,), writes=(('ps', sl),))
                    pa, pb = ps_AB(sl)
                    def cp_in(e):
                        e.copy(out=XP3[:, 0:4, 1:257], in_=pa.rearrange("p (s t) -> p s t", s=4))
                        return e.copy(out=XP3[:, 4, 1:257], in_=pb)
                    p.op('act', cp_in, reads=(('ps', sl),), writes=('XP',))
                    def halo(e):
                        e.tensor_scalar(out=XP3[:, 1:4, 0:1], in0=XP3[:, 0:3, 256:257], scalar1=CONT[:, 0:1], scalar2=None, op0=ALU.mult)
                        return e.tensor_scalar(out=XP3[:, 0:3, 257:258], in0=XP3[:, 1:4, 1:2], scalar1=CONT[:, 0:1], scalar2=None, op0=ALU.mult)
                    p.op('dve', halo, reads=('XP', 'CONT'), writes=('XP',))
                    def conv(e):
                        e.tensor_scalar(out=CACC, in0=XP3[:, :, 0:256], scalar1=HYC[:, ci, 0:1], scalar2=HYC[:, ci, 3:4],
                                        op0=ALU.mult, op1=ALU.add)
                        e.scalar_tensor_tensor(out=CACC, in0=XP3[:, :, 1:257], scalar=HYC[:, ci, 1:2], in1=CACC, op0=ALU.mult, op1=ALU.add)
                        return e.scalar_tensor_tensor(out=dst, in0=XP3[:, :, 2:258], scalar=HYC[:, ci, 2:3], in1=CACC,
                                                      op0=ALU.mult, op1=ALU.add)
                    p.op('dve', conv, reads=('XP', 'HYC'), writes=(('TMP', 0),) + dkeys)

                def seg5(ap):
                    return ap.rearrange("p (s t) -> p s t", s=5)

                def build_ut(srcs, skeys):
                    for tc in range(T // 128):
                        bank, bkey = (PSXB, 'psx') if tc % 2 == 0 else (PSYB, 'psy')
                        def tr(e, tc=tc, bank=bank):
                            ins = None
                            for sub in range(2):
                                ins = e.transpose(bank[:, sub * 128:(sub + 1) * 128], srcs[sub][:, tc * 128:(tc + 1) * 128], IDB[:])
                            return ins
                        p.op('pe', tr, reads=skeys + ('IDB',), writes=(bkey,))
                        p.op('act', lambda e, tc=tc, bank=bank: e.copy(out=UT[:, tc, :], in_=bank[:, 0:256]),
                             reads=(bkey,), writes=(('TMP', 0),))

                def build_ft(o, cpi):
                    ch0 = o * 1024 + cpi * 256
                    def ldf(eng, sm):
                        eng.dma_start(out=W3S[:], in_=dr["hy_w3"][o_idx][:, ch0:ch0 + 256]).then_inc(sm, 16)
                        eng.dma_start(out=DECS, in_=dr["decb"][:, ch0:ch0 + 256]).then_inc(sm, 16)
                    p.dma('pool', ldf, 2, 'ldft', writes=('W3S', ('RS', 3)))
                    p.op('act', lambda e: e.activation(out=DECS, in_=DECS, func=AF.Abs),
                         reads=(('RS', 3),), writes=(('RS', 3),))
                    for nc_ in range(10):
                        zc0 = nc_ * 128
                        p.op('pe', lambda e, zc0=zc0: e.matmul(PSX[:, 0:256], Z2T[:, zc0:zc0 + 128], W3S[:], start=True, stop=True),
                             reads=('Z2T', 'W3S'), writes=('psx',))
                        p.op('act', lambda e, nc_=nc_: e.activation(out=ES, in_=DECS, func=AF.Exp, scale=NEGT[:, nc_:nc_ + 1]),
                             reads=(('RS', 3), 'NEGT'), writes=(('RS', 4),))
                        p.op('dve', lambda e, nc_=nc_: e.tensor_tensor(out=FT[:, nc_, :], in0=PSX[:, 0:256], in1=ES, op=ALU.mult),
                             reads=('psx', ('RS', 4)), writes=(('TMP', 1),))
                        p.op('act', lambda e, nc_=nc_: e.activation(out=ABSF, in_=FT[:, nc_, :], func=AF.Abs),
                             reads=(('TMP', 1),), writes=('XP',))
                        first = nc_ in (0, 8)
                        last = nc_ in (7, 9)
                        p.op('pe', lambda e, first=first, last=last: e.matmul(PSY[:, 0:256], ONES[:], ABSF, start=first, stop=last),
                             reads=('XP', 'ONES'), writes=('psy',))
                        if last:
                            RS = RSA if nc_ == 7 else RSB
                            rk = ('RS', 1) if nc_ == 7 else ('RS', 2)
                            def rinv(e, RS=RS):
                                e.activation(out=RS, in_=PSY[:, 0:256], func=AF.Ln)
                                return e.activation(out=RS, in_=RS, func=AF.Exp, scale=-1.0)
                            p.op('act', rinv, reads=('psy',), writes=(rk,))

                def forward(srckeys):
                    for blk in range(5):
                        if blk < 4:
                            fsrc = dr["ffA"][:, blk * 512:(blk + 1) * 512].rearrange("(k p) c -> p k c", p=128)
                            gsrc = dr["gtA"][:, blk * 512:(blk + 1) * 512].rearrange("(k p) c -> p k c", p=128)
                            nk, k0, RS, rk = 8, 0, RSA, ('RS', 1)
                        else:
                            fsrc = dr["ffB"].rearrange("(k p) c -> p k c", p=128)
                            gsrc = dr["gtB"].rearrange("(k p) c -> p k c", p=128)
                            nk, k0, RS, rk = 2, 8, RSB, ('RS', 2)
                        fv, fk = ws.next(fsrc, nk, 512)
                        gv, gk = ws.next(gsrc, nk, 512)
                        if dry:
                            continue
                        for ccl in range(4):
                            cc = blk * 4 + ccl
                            bu, bg_ = [(6, 7), (0, 1), (2, 3), (4, 5)][cc % 4]
                            PU, PG = PS[:, bu * 512:bu * 512 + 256], PS[:, bg_ * 512:bg_ * 512 + 256]
                            ku, kg = ('bank', bu), ('bank', bg_)
                            def fmm(e, fv=fv, ccl=ccl, nk=nk, k0=k0, PU=PU):
                                ins = None
                                for k in range(nk):
                                    ins = e.matmul(PU, fv[:, k, ccl * 128:(ccl + 1) * 128], UT[:, k0 + k, :],
                                                   start=(k == 0), stop=(k == nk - 1))
                                return ins
                            p.op('pe', fmm, reads=(fk, ('TMP', 0)), writes=(ku,))
                            def gmm(e, gv=gv, ccl=ccl, nk=nk, k0=k0, PG=PG):
                                ins = None
                                for k in range(nk):
                                    ins = e.matmul(PG, gv[:, k, ccl * 128:(ccl + 1) * 128], FT[:, k0 + k, :],
                                                   start=(k == 0), stop=(k == nk - 1))
                                return ins
                            p.op('pe', gmm, reads=(gk, ('TMP', 1)), writes=(kg,))
                            KF2 = KFS if cc % 2 == 0 else ES
                            kk2 = ('RS', 0) if cc % 2 == 0 else ('RS', 4)
                            p.op('dve', lambda e, RS=RS, PG=PG, KF2=KF2: e.tensor_tensor(out=KF2, in0=PG, in1=RS, op=ALU.mult),
                                 reads=(kg, rk), writes=(kk2,))
                            p.op('dve', lambda e, cc=cc, PU=PU, KF2=KF2: e.tensor_tensor(out=Wsp[:, cc, :], in0=PU, in1=KF2, op=ALU.mult),
                                 reads=(ku, kk2), writes=Wk)

                def inverse():
                    stages = []
                    for th in range(2):
                        for half in range(2):
                            src = dr["fiA"][half * 1024:(half + 1) * 1024, th * 512:(th + 1) * 512].rearrange("(k p) c -> p k c", p=128)
                            stages.append((th, half, src))
                    first = True
                    for th, half, src in stages:
                        iv, ik = ws.next(src, 8, 512)
                        if dry:
                            continue
                        def imm(e, iv=iv, th=th, half=half):
                            ins = None
                            for sub in range(2):
                                pt = ps_tiles(sub)[th]
                                for ccl in range(8):
                                    cc = half * 8 + ccl
                                    ins = e.matmul(pt, Wsp[:, cc, sub * 128:(sub + 1) * 128], iv[:, ccl, :],
                                                   start=(cc == 0), stop=(cc == 15))
                            return ins
                        p.op('pe', imm, reads=Wk + (ik,), writes=(('ps', 0), ('ps', 1)))
                    bv, bk = ws.next(dr["fiB"].rearrange("(k p) c -> p k c", p=128), 4, 256)
                    if dry:
                        return
                    def imb(e):
                        ins = None
                        for sub in range(2):
                            pt = ps_tiles(sub)[2]
                            for ccl in range(4):
                                ins = e.matmul(pt, Wsp[:, 16 + ccl, sub * 128:(sub + 1) * 128], bv[:, ccl, :],
                                               start=(ccl == 0), stop=(ccl == 3))
                        return ins
                    p.op('pe', imb, reads=Wk + (bk,), writes=(('ps', 0), ('ps', 1)))

                import os
                for cpi in range(0 if not os.environ.get('NOHY') else 4, 4):
                    for sub in range(2):
                        c = 2 * cpi + sub
                        conv_chunk(2048 + c * 128, 16 + c, seg5(VB[:, sub, :]) if not dry else None, (('ACTB', 4 + sub),))
                        conv_chunk(c * 128, c, seg5(X1B[:, sub, :]) if not dry else None, (('ACTB', 6 + sub),))
                    if not dry:
                        build_ut([VB[:, 0, :], VB[:, 1, :]], VBk)
                        build_ft(0, cpi)
                    forward(VBk)
                    inverse()
                    if not dry:
                        slotc[0] = 0
                        for sub in range(2):
                            c = 2 * cpi + sub
                            pa, pb = ps_AB(sub)
                            def z1a(e, sub=sub, c=c, pa=pa, pb=pb):
                                e.scalar_tensor_tensor(out=TT[:, 0:TA], in0=VB[:, sub, 0:TA], scalar=HSK[:, 0, c:c + 1], in1=pa,
                                                       op0=ALU.mult, op1=ALU.add)
                                return e.scalar_tensor_tensor(out=TT[:, TA:T], in0=VB[:, sub, TA:T], scalar=HSK[:, 0, c:c + 1], in1=pb,
                                                              op0=ALU.mult, op1=ALU.add)
                            p.op('dve', z1a, reads=(('ACTB', 4 + sub), ('ps', sub), 'HSK'), writes=(('TMP', 1),))
                            p.op('dve', lambda e, sub=sub: e.tensor_tensor(out=X1B[:, sub, :], in0=TT, in1=X1B[:, sub, :], op=ALU.mult),
                                 reads=(('TMP', 1), ('ACTB', 6 + sub)), writes=(('ACTB', 6 + sub),))
                    for sub in range(2):
                        c = 2 * cpi + sub
                        conv_chunk(1024 + c * 128, 8 + c, seg5(VB[:, sub, :]) if not dry else None, (('ACTB', 4 + sub),))
                    if not dry:
                        build_ut([X1B[:, 0, :], X1B[:, 1, :]], X1k)
                        build_ft(1, cpi)
                    forward(X1k)
                    inverse()
                    if not dry:
                        slotc[0] = 0
                    if not dry:
                        for sub in range(2):
                            c = 2 * cpi + sub
                            pa, pb = ps_AB(sub)
                            def z2a(e, sub=sub, c=c, pa=pa, pb=pb):
                                e.scalar_tensor_tensor(out=TT[:, 0:TA], in0=X1B[:, sub, 0:TA], scalar=HSK[:, 1, c:c + 1], in1=pa,
                                                       op0=ALU.mult, op1=ALU.add)
                                return e.scalar_tensor_tensor(out=TT[:, TA:T], in0=X1B[:, sub, TA:T], scalar=HSK[:, 1, c:c + 1], in1=pb,
                                                              op0=ALU.mult, op1=ALU.add)
                            p.op('dve', z2a, reads=(('ACTB', 6 + sub), ('ps', sub), 'HSK'), writes=(('TMP', 1),))
                            p.op('dve', lambda e, sub=sub: e.tensor_tensor(out=X1B[:, sub, :], in0=TT, in1=VB[:, sub, :], op=ALU.mult),
                                 reads=(('TMP', 1), ('ACTB', 4 + sub)), writes=(('ACTB', 6 + sub),))
                    wout_incr(dr["cd_w_out"][o_idx][cpi * 256:(cpi + 1) * 256, :],
                              [X1B[:, 0, :], X1B[:, 1, :]] if not dry else [None, None], [('ACTB', 6), ('ACTB', 7)], l)


                PLT = TMP[:, 0, :].bitcast(BF16).rearrange("p (c d) -> p c d", c=10)
                DMB = TMP[:, 1, :].bitcast(BF16).rearrange("p (c d) -> p c d", c=2)
                POOLED = SQ[:, 0, :]
                POOLED1 = XP[:].rearrange("p s t -> p (s t)")[:, 0:640].bitcast(BF16)
                p.sertags = ('mmd',)
                for g in range(0 if not os.environ.get('NOPOOL') else 4, 4):
                    psrc = win[:, 3072 + g * 256:3072 + (g + 1) * 256].rearrange("(k p) c -> p k c", p=128)
                    pv, pk = ws.next(psrc, 16, 256)
                    if not dry:
                        for tc in range(T // 128):
                            bank, bkey = (PSX, 'psx') if tc % 2 == 0 else (PSY, 'psy')
                            def mmp(e, tc=tc, bank=bank, pv=pv):
                                ins = None
                                for kc in range(NCH):
                                    ins = e.matmul(bank[:, 0:256], H[:, kc, tc * 128:(tc + 1) * 128], pv[:, kc, :],
                                                   start=(kc == 0), stop=(kc == NCH - 1))
                                return ins
                            p.op('pe', mmp, reads=Hkeys + (pk,), writes=(bkey,), tag='mmp')
                            p.op('act', lambda e, tc=tc, bank=bank: e.copy(out=PLT[:, tc, :], in_=bank[:, 0:256]),
                                 reads=(bkey,), writes=(('TMP', 0),), tag='plt')
                    dv, dk = ws.next(dr["poolD"][g], 30, 128)
                    if not dry:
                        for cl in range(2):
                            sl = new_slot()
                            pts = ps_tiles(sl)
                            def mmd(e, cl=cl, pts=pts, dv=dv):
                                ins = None
                                for tcn in range(10):
                                    dstp = pts[tcn // 4][:, (tcn % 4) * 128:(tcn % 4 + 1) * 128]
                                    offs = [o for o in (-1, 0, 1) if 0 <= tcn + o < 10]
                                    for oi, o in enumerate(offs):
                                        st = (oi == 0) and (tcn in (0, 4, 8))
                                        sp = (oi == len(offs) - 1) and (tcn in (3, 7, 9))
                                        ins = e.matmul(dstp, PLT[:, tcn + o, cl * 128:(cl + 1) * 128], dv[:, tcn * 3 + o + 1, :],
                                                       start=st, stop=sp, skip_group_check=True)
                                return ins
                            p.op('pe', mmd, reads=(('TMP', 0), dk), writes=(('ps', sl),), tag='mmd')
                            pa, pb = ps_AB(sl)
                            def cpd(e, cl=cl, pa=pa, pb=pb):
                                e.copy(out=DMB[:, cl, 0:TA], in_=pa)
                                return e.copy(out=DMB[:, cl, TA:T], in_=pb)
                            p.op('act', cpd, reads=(('ps', sl),), writes=(('TMP', 1),), tag='cpd')
                    wv_, wk_ = ws.next(dr["pool_w"][o_idx][g].rearrange("(k p) c -> p k c", p=128), 2, 256)
                    for j in range(2):
                        c = 2 * g + j
                        if not dry:
                            sl = new_slot()
                            pts = ps_tiles(sl)
                            def mmw(e, j=j, pts=pts, wv_=wv_):
                                ins = None
                                for i in range(2):
                                    for ti, (t0, tw) in enumerate(tok_tiles()):
                                        ins = e.matmul(pts[ti], wv_[:, i, j * 128:(j + 1) * 128], DMB[:, i, t0:t0 + tw],
                                                       start=(i == 0), stop=(i == 1))
                                return ins
                            p.op('pe', mmw, reads=(('TMP', 1), wk_), writes=(('ps', sl),), tag='mmw')
                            pa, pb = ps_AB(sl)
                            PD = POOLED if j == 0 else POOLED1
                            def psc(e, c=c, pa=pa, pb=pb, PD=PD):
                                e.activation(out=PD[:, 0:TA], in_=pa, func=AF.Identity, scale=PSCL[:, c:c + 1])
                                return e.activation(out=PD[:, TA:T], in_=pb, func=AF.Identity, scale=PSCL[:, c:c + 1])
                            p.op('act', psc, reads=(('ps', sl), 'PSCL'), writes=((('SQ', 0),) if j == 0 else ('XP',)), tag='psc')
                    wout_incr(dr["cd_w_out"][o_idx][1024 + g * 256:1024 + (g + 1) * 256, :],
                              [POOLED, POOLED1] if not dry else [None, None], [('SQ', 0), 'XP'], l)

            modulation(0, 0, 3)
            mod_derive(0, 0)
            if stop >= 1:
                ffn(0, 0, 0, bg=mod_tasks(0, 3, 9))
            else:
                modulation(0, 3, 9)
            mod_derive(0, 1)
            mod_derive(0, 2)
            if stop >= 2:
                mixer_even(0)
            if stop >= 3:
                ffn(0, 2, 1, bg=mod_tasks(1, 0, 9))
                for j in range(3):
                    mod_derive(1, j)
            if stop >= 4:
                ffn(1, 0, 0)
            if stop >= 5:
                mixer_odd(1)
            if stop >= 6:
                ffn(1, 2, 1)

            yv = dr["yT"].rearrange("(c p) t -> p c t", p=128)
            for q in range(4):
                def fn(eng, sm, q=q):
                    eng.dma_start(out=yv[:, 4 * q:4 * q + 4, :], in_=X[:, 4 * q:4 * q + 4, :]).then_inc(sm, 16)
                p.dma('sp', fn, 1, 'sty', reads=tuple(('Xc', c) for c in range(4 * q, 4 * q + 4)), is_output=True)
            def fnm(eng, sm):
                eng.dma_start(out=dr["modout"], in_=MOD[:].rearrange("p a b c d -> p (a b c d)")).then_inc(sm, 16)
            p.dma('sp', fnm, 1, 'stm', reads=tuple(('MOD', 0, q) for q in range(9)), is_output=True)

            if dry:
                plan = ws.rec
            else:
                p.emit()
    return nc


def core_tokens(inputs, core):
    xp, xs = inputs['x_prompt'], inputs['x_sample']
    if core < 2:
        return np.concatenate([xs[core], xp[core]], axis=0), True
    segs = [xp[2 + (core - 2) * 5 + j] for j in range(5)]
    return np.concatenate(segs, axis=0), False


def make_in_maps(inputs, stop=99):
    f32 = np.float32
    normg = np.ascontiguousarray(inputs['norm_g'].reshape(6, NCH, 128).transpose(2, 0, 1)).astype(f32)
    bmod = np.ascontiguousarray(inputs['b_mod'].reshape(2, 9 * NCH, 128).transpose(2, 0, 1)).astype(f32)
    qkn = np.stack([inputs['ab_q_norm'][0], inputs['ab_k_norm'][0]], axis=1).astype(f32)
    rotm = np.zeros((128, 128), f32)
    for i in range(64):
        rotm[2 * i + 1, 2 * i] = -1.0
        rotm[2 * i, 2 * i + 1] = 1.0
    tt = np.arange(TA)
    rr, cc = (tt // 64).astype(np.float64), (tt % 64).astype(np.float64)
    inv = 10000.0 ** (-np.arange(0, 64, 2, dtype=np.float64) / 64)
    ang = np.concatenate([rr[:, None] * inv, cc[:, None] * inv], axis=-1)
    cossin_long = np.stack([np.repeat(np.cos(ang), 2, axis=1).T, np.repeat(np.sin(ang), 2, axis=1).T], axis=1).astype(f32)
    cossin_id = np.stack([np.ones((128, TA)), np.zeros((128, TA))], axis=1).astype(f32)
    maskb_long = np.zeros((128, 48), f32)
    maskb_prompt = np.full((128, 48), -30000.0, f32)
    for sc in range(4, 12):
        maskb_prompt[:, sc * 4 + (sc - 4) // 2] = 0.0
    bf = ml_dtypes.bfloat16
    def dft_consts(L):
        j = np.arange(2 * L)
        k = np.where(j <= L, j, j - L)
        is_sin = j > L
        sidx = np.arange(L)
        arg = np.pi * np.outer(k, sidx) / L
        Cm = np.where(is_sin[:, None], np.sin(arg), np.cos(arg))
        w = np.where((k == 0) | (k == L), 1.0, 2.0) / (2 * L)
        nf = np.where(sidx == 0, 1.0, 2.0)
        G = w[:, None] * nf[None, :] * np.cos(arg)
        return Cm.T.copy(), Cm.copy(), G.T.copy()
    def z0feat(L):
        n = np.arange(L, dtype=np.float64)
        t = n / max(L - 1, 1)
        bands = np.linspace(1e-4, 15.0, 16)
        f = 2.0 * np.pi * n[:, None] * bands[None, :] / L
        return np.concatenate([t[:, None], np.cos(f), -np.sin(f)], axis=-1), t
    FF1k, FI1k, GT1k = dft_consts(1024)
    FF256, FI256, GT256 = dft_consts(256)
    ffA_long, fiA_long, gtA_long = FF1k.astype(bf), FI1k.astype(bf), GT1k.astype(bf)
    ffA_p = np.zeros((1024, 2048)); fiA_p = np.zeros((2048, 1024)); gtA_p = np.zeros((1024, 2048))
    for sgi in range(4):
        ffA_p[sgi * 256:(sgi + 1) * 256, sgi * 512:(sgi + 1) * 512] = FF256
        fiA_p[sgi * 512:(sgi + 1) * 512, sgi * 256:(sgi + 1) * 256] = FI256
        gtA_p[0:256, sgi * 512:(sgi + 1) * 512] = GT256
    ffA_prompt, fiA_prompt, gtA_prompt = ffA_p.astype(bf), fiA_p.astype(bf), gtA_p.astype(bf)
    ffB, fiB, gtB = FF256.astype(bf), FI256.astype(bf), GT256.astype(bf)
    z1k, t1k = z0feat(1024)
    z256, t256 = z0feat(256)
    z0T_long = np.concatenate([z1k.T, z256.T], axis=1).astype(f32)
    z0T_prompt = np.concatenate([z256.T, np.zeros((33, 768)), z256.T], axis=1).astype(f32)
    negt_long = np.concatenate([-t1k.reshape(8, 128).T, -t256.reshape(2, 128).T], axis=1).astype(f32)
    negt_prompt = np.concatenate([-t256.reshape(2, 128).T, np.full((128, 6), -1.0e4), -t256.reshape(2, 128).T], axis=1).astype(f32)
    hyfb = np.stack([inputs['hy_b1'][0], inputs['hy_b2'][0], inputs['hy_freq'][0, 0], inputs['hy_freq'][0, 1]], axis=1).astype(f32)
    hyc = np.zeros((128, 24, 4), f32)
    hyc[:, :, 0:3] = inputs['hy_conv_w'][0].reshape(3, 24, 128).transpose(2, 1, 0)
    hyc[:, :, 3] = inputs['hy_conv_b'][0].reshape(24, 128).T
    hsk = np.ascontiguousarray(inputs['hy_skip'][0].reshape(2, 8, 128).transpose(2, 0, 1)).astype(f32)
    decb = np.ascontiguousarray(np.broadcast_to(inputs['hy_decay'][0][None, :], (128, 2048))).astype(f32)
    def pool_blocks(bounds):
        out = np.zeros((4, 128, 30, 128))
        for gi, w in enumerate((2, 4, 8, 16)):
            DT = np.zeros((T, T))
            for (a, b) in bounds:
                L = b - a
                for t in range(L):
                    lo, hi = max(t - w // 2, 0), min(t + w // 2, L)
                    DT[a + lo:a + hi, a + t] += 1.0 / (hi - lo)
                    DT[a + t, a + t] -= 1.0
            for tcn in range(10):
                for o in (-1, 0, 1):
                    if 0 <= tcn + o < 10:
                        out[gi, :, tcn * 3 + o + 1, :] = DT[(tcn + o) * 128:(tcn + o + 1) * 128, tcn * 128:(tcn + 1) * 128]
        return out.astype(bf)
    poolD_long = pool_blocks([(0, 1024), (1024, 1280)])
    poolD_prompt = pool_blocks([(i * 256, (i + 1) * 256) for i in range(5)])
    pscl = np.ascontiguousarray(inputs['pool_scale'][0].reshape(8, 128).T).astype(f32)
    gw_host = np.ascontiguousarray(
        inputs['lru_gate_w'][0].reshape(4, 8, 128, 128).transpose(1, 2, 0, 3)).astype(f32)
    lruc = np.zeros((128, 8, 10), f32)
    lruc[:, :, 0:4] = inputs['lru_conv_w'][0].reshape(4, 8, 128).transpose(2, 1, 0)
    lruc[:, :, 4] = inputs['lru_conv_b'][0].reshape(8, 128).T
    lruc[:, :, 5:9] = inputs['lru_gate_b'][0].reshape(4, 8, 128).transpose(2, 1, 0)
    lrul = np.ascontiguousarray(inputs['lru_lambda'][0].reshape(2, 8, 128).transpose(2, 0, 1)).astype(f32)
    maps = []
    for core in range(NCORES):
        xt, is_long = core_tokens(inputs, core)
        condA = inputs['c'][core] if is_long else inputs['c_ctx']
        condB = inputs['c_ctx']
        cond = np.stack([condA.reshape(NCH, 128).T, condB.reshape(NCH, 128).T], axis=1)
        m = {
            "xT": np.ascontiguousarray(xt.T),
            "cond": np.ascontiguousarray(cond).astype(f32),
            "normg": normg,
            "bmod": bmod,
            "w_mod": inputs['w_mod'],
            "ffn_w13": inputs['ffn_w13'],
            "ffn_w2": inputs['ffn_w2'],
            "ab_w_in": inputs['ab_w_in'],
            "knb": np.ascontiguousarray(np.broadcast_to(inputs['ab_k_norm'][0][None, :], (128, 128))).astype(f32),
            "lru_gate_w": gw_host,
            "ab_w_out": inputs['ab_w_out'],
            "qkn": qkn, "rotm": rotm,
            "cossin": cossin_long if is_long else cossin_id,
            "ckT": (np.ascontiguousarray(inputs['cache_k'][core, 0].transpose(2, 1, 0)).astype(f32)
                    if is_long else np.zeros((128, 2, 512), f32)),
            "cv": (np.ascontiguousarray(inputs['cache_v'][core, 0].reshape(4, 128, 256).transpose(1, 0, 2)).astype(f32)
                   if is_long else np.zeros((128, 4, 256), f32)),
            "maskb": maskb_long if is_long else maskb_prompt,
            "cd_w_in": inputs['cd_w_in'], "cd_w_out": inputs['cd_w_out'],
            "hy_w1": inputs['hy_w1'], "hy_w2": inputs['hy_w2'], "hy_w3": inputs['hy_w3'],
            "hyfb": hyfb, "hyc": hyc, "hsk": hsk, "decb": decb,
            "z0T": z0T_long if is_long else z0T_prompt,
            "negt": negt_long if is_long else negt_prompt,
            "ffA": ffA_long if is_long else ffA_prompt,
            "gtA": gtA_long if is_long else gtA_prompt,
            "fiA": fiA_long if is_long else fiA_prompt,
            "ffB": ffB, "gtB": gtB, "fiB": fiB,
            "poolD": poolD_long if is_long else poolD_prompt,
            "pool_w": inputs['pool_w'], "pscl": pscl,
            "lruc": lruc, "lrul": lrul,
            "lruh": (np.stack([inputs['state_lru_fwd'][core, 0].reshape(8, 128).T,
                               inputs['state_lru_bwd'][core, 0].reshape(8, 128).T], axis=1).astype(f32)
                     if is_long else np.zeros((128, 2, 8), f32)),
            "cont": np.full((128, 1), 1.0 if is_long else 0.0, f32),
        }
        maps.append(m)
    return maps


def kernel(**inputs):
    inputs = {k: np.asarray(v) for k, v in inputs.items()}
    nc = build()
    maps = make_in_maps(inputs)
    res = run_bass_kernel_spmd(nc, maps, core_ids=list(range(NCORES)))
    f32 = np.float32
    y_prompt = np.zeros((32, 256, D), f32)
    y_sample = np.zeros((2, 1024, D), f32)
    nk = np.zeros((32, 1, 256, 2, 128), f32)
    nv = np.zeros((32, 1, 256, 2, 128), f32)
    sf = np.zeros((32, 1, 1024), f32)
    sb = np.zeros((32, 1, 1024), f32)
    for core in range(NCORES):
        r = res.results[core]
        y = np.ascontiguousarray(r["yT"].T)
        kvo = r["kvo"]
        sto = r["sto"].reshape(128, 8, 5, 2)
        if core < 2:
            y_sample[core] = y[:1024]
            segs = [(4, core)]
        else:
            segs = [(j, 2 + (core - 2) * 5 + j) for j in range(5)]
        for j, b in segs:
            y_prompt[b] = y[j * 256:(j + 1) * 256]
            nk[b, 0] = kvo[j * 256:(j + 1) * 256, 0:256].reshape(256, 2, 128)
            nv[b, 0] = kvo[j * 256:(j + 1) * 256, 256:512].reshape(256, 2, 128)
            sf[b, 0] = sto[:, :, j, 0].T.reshape(1024)
            sb[b, 0] = sto[:, :, j, 1].T.reshape(1024)
    return (y_prompt, y_sample, nk, nv, sf, sb)
```
